# Optimizing a Trainium2 kernel written in Bass

```python
import jax, jax.numpy as jnp
from jax import lax
import numpy as np

D_MODEL = 1024
BATCH = 4
SEQ = 4096
DEPTH = 1

GRID_W = 64
CTX_LEN = 256
N_FOURIER_GROUPS = 4
FOURIER_GROUP_W = 128
FOURIER_W = N_FOURIER_GROUPS * FOURIER_GROUP_W
N_HEADS = 8
QK_NOPE = 64
QK_ROPE = 32
HEAD_QK = QK_NOPE + QK_ROPE
V_DIM = 64
Q_LORA = 256
KV_LORA = 128
MLA_W = N_HEADS * V_DIM
N_BRANCH = 2
D_FF = 4 * D_MODEL
D_IN = FOURIER_W + Q_LORA + KV_LORA + QK_ROPE + N_BRANCH * D_MODEL
ROPE_THETA = 10000.0
Q_BLOCK = 128
EPS = 1e-6

kernel_name = "hybrid_fourier_mla_dit_block"


def rmsnorm(x, g):
    xf = x.astype(jnp.float32)
    y = xf * lax.rsqrt(jnp.mean(xf * xf, axis=-1, keepdims=True) + EPS)
    return (y * g.astype(jnp.float32)).astype(x.dtype)


def modulate(h, shift, scale):
    return h * (1 + scale) + shift


def axial_rope_tables(n):
    rows = n // GRID_W
    row = jnp.repeat(jnp.arange(rows, dtype=jnp.float32), GRID_W)
    col = jnp.tile(jnp.arange(GRID_W, dtype=jnp.float32), rows)
    n_freq = QK_ROPE // 4
    freqs = ROPE_THETA ** (-jnp.arange(n_freq, dtype=jnp.float32) / n_freq)
    ang_r = row[:, None] * freqs[None, :]
    ang_c = col[:, None] * freqs[None, :]
    ang = jnp.concatenate([ang_r, ang_r, ang_c, ang_c], axis=-1)
    return jnp.cos(ang), jnp.sin(ang)


def apply_rope(x, cos, sin):
    a1, a2, b1, b2 = jnp.split(x, 4, axis=-1)
    rot = jnp.concatenate([-a2, a1, -b2, b1], axis=-1)
    return x * cos[None, :, None, :].astype(x.dtype) + rot * sin[None, :, None, :].astype(x.dtype)


def split_proj(p):
    o1 = FOURIER_W
    o2 = o1 + Q_LORA
    o3 = o2 + KV_LORA
    o4 = o3 + QK_ROPE
    return p[..., :o1], p[..., o1:o2], p[..., o2:o3], p[..., o3:o4], p[..., o4:]


def fourier_mix(f):
    B, N, _ = f.shape
    fg = f.reshape(B, N, N_FOURIER_GROUPS, FOURIER_GROUP_W).astype(jnp.float32)
    y = jnp.fft.fft2(fg, axes=(1, 3), norm="ortho").real
    return y.reshape(B, N, FOURIER_W).astype(f.dtype)


def mla_kv(kv_lat, k_rope, kv_norm_g, w_kvb, k_gain, rope):
    B, N, _ = kv_lat.shape
    kv = (rmsnorm(kv_lat, kv_norm_g) @ w_kvb).reshape(B, N, N_HEADS, QK_NOPE + V_DIM)
    k_nope, v = kv[..., :QK_NOPE], kv[..., QK_NOPE:]
    k_r = jnp.broadcast_to(k_rope[:, :, None, :], (B, N, N_HEADS, QK_ROPE))
    k = rmsnorm(jnp.concatenate([k_nope, k_r], axis=-1), k_gain)
    if rope is not None:
        k = jnp.concatenate([k[..., :QK_NOPE], apply_rope(k[..., QK_NOPE:], *rope)], axis=-1)
    return k, v


def mla_q(q_lat, q_norm_g, w_qb, q_gain, rope):
    B, N, _ = q_lat.shape
    q = (rmsnorm(q_lat, q_norm_g) @ w_qb).reshape(B, N, N_HEADS, HEAD_QK)
    q = rmsnorm(q, q_gain)
    if rope is not None:
        q = jnp.concatenate([q[..., :QK_NOPE], apply_rope(q[..., QK_NOPE:], *rope)], axis=-1)
    return q


def latent_attention(q, k_all, v_all):
    B, S, H, Dh = q.shape
    nb = S // Q_BLOCK
    qb = q.reshape(B, nb, Q_BLOCK, H, Dh).transpose(1, 0, 2, 3, 4)
    scale = HEAD_QK ** -0.5

    def one_block(qblk):
        s = jnp.einsum('bqhd,bkhd->bhqk', qblk, k_all, preferred_element_type=jnp.float32) * scale
        p = jax.nn.softmax(s, axis=-1).astype(v_all.dtype)
        return jnp.einsum('bhqk,bkhd->bqhd', p, v_all)

    o = lax.map(one_block, qb)
    return o.transpose(1, 0, 2, 3, 4).reshape(B, S, H * V_DIM)


def context_attention(q, k, v):
    B, N = q.shape[0], q.shape[1]
    s = jnp.einsum('bqhd,bkhd->bhqk', q, k, preferred_element_type=jnp.float32) * (HEAD_QK ** -0.5)
    p = jax.nn.softmax(s, axis=-1).astype(v.dtype)
    return jnp.einsum('bhqk,bkhd->bqhd', p, v).reshape(B, N, N_HEADS * V_DIM)


def merge_branches(f_mix, att, gates, w_fourier, w_mla_o, b_gate, w_out):
    g = jax.nn.sigmoid(gates + b_gate)
    g_f, g_a = g[..., :D_MODEL], g[..., D_MODEL:]
    y = g_f * (f_mix @ w_fourier) + g_a * (att @ w_mla_o)
    return y @ w_out


def sq_relu_mlp(h, w_up, w_down):
    return jnp.square(jax.nn.relu(h @ w_up)) @ w_down


def setup_inputs(seed: int = 0) -> dict:
    key = jax.random.key(seed)
    ks = jax.random.split(key, 24)
    D, L = D_MODEL, DEPTH

    def w(k, shape, fan_in, mult=1.0):
        return jax.random.normal(k, shape, jnp.float32) * (mult * fan_in ** -0.5)

    def gain(k, shape):
        return 1.0 + 0.05 * jax.random.normal(k, shape, jnp.float32)

    return {
        "x": jax.random.normal(ks[0], (BATCH, SEQ, D), jnp.float32),
        "c": jax.random.normal(ks[1], (BATCH, D), jnp.float32),
        "ctx": jax.random.normal(ks[2], (BATCH, CTX_LEN, D), jnp.float32),
        "c_ctx": jax.random.normal(ks[3], (D,), jnp.float32),
        "ada_w": w(ks[4], (L, D, 6 * D), D, 0.5),
        "ada_b": 0.02 * jax.random.normal(ks[5], (L, 6 * D), jnp.float32),
        "norm1_g": gain(ks[6], (L, D)),
        "norm2_g": gain(ks[7], (L, D)),
        "w_in": w(ks[8], (L, D, D_IN), D),
        "b_gate": 0.02 * jax.random.normal(ks[9], (L, N_BRANCH * D), jnp.float32),
        "w_fourier": w(ks[10], (L, FOURIER_W, D), FOURIER_W),
        "q_norm_g": gain(ks[11], (L, Q_LORA)),
        "w_qb": w(ks[12], (L, Q_LORA, N_HEADS * HEAD_QK), Q_LORA),
        "kv_norm_g": gain(ks[13], (L, KV_LORA)),
        "w_kvb": w(ks[14], (L, KV_LORA, N_HEADS * (QK_NOPE + V_DIM)), KV_LORA),
        "q_gain": gain(ks[15], (L, HEAD_QK)),
        "k_gain": gain(ks[16], (L, HEAD_QK)),
        "w_mla_o": w(ks[17], (L, MLA_W, D), MLA_W),
        "w_out": w(ks[18], (L, D, D), D),
        "w_up": w(ks[19], (L, D, D_FF), D),
        "w_down": w(ks[20], (L, D_FF, D), D_FF),
    }


def reference(x, c, ctx, c_ctx, ada_w, ada_b, norm1_g, norm2_g, w_in, b_gate, w_fourier, q_norm_g, w_qb,
              kv_norm_g, w_kvb, q_gain, k_gain, w_mla_o, w_out, w_up, w_down):
    rope = axial_rope_tables(x.shape[1])
    for l in range(DEPTH):
        last = l == DEPTH - 1
        mod_x = jax.nn.silu(c) @ ada_w[l] + ada_b[l]
        sh1, sc1, g1, sh2, sc2, g2 = jnp.split(mod_x[:, None, :], 6, axis=-1)
        mod_c = jax.nn.silu(c_ctx) @ ada_w[l] + ada_b[l]
        csh1, csc1, cg1, csh2, csc2, cg2 = jnp.split(mod_c, 6, axis=-1)

        hc = modulate(rmsnorm(ctx, norm1_g[l]), csh1, csc1)
        pc = hc @ w_in[l]
        fc, qlc, kvlc, krc, gc = split_proj(pc)
        kc, vc = mla_kv(kvlc, krc, kv_norm_g[l], w_kvb[l], k_gain[l], None)

        hx = modulate(rmsnorm(x, norm1_g[l]), sh1, sc1)
        px = hx @ w_in[l]
        fx, qlx, kvlx, krx, gx = split_proj(px)
        kx, vx = mla_kv(kvlx, krx, kv_norm_g[l], w_kvb[l], k_gain[l], rope)
        qx = mla_q(qlx, q_norm_g[l], w_qb[l], q_gain[l], rope)
        att_x = latent_attention(qx, jnp.concatenate([kx, kc], axis=1), jnp.concatenate([vx, vc], axis=1))
        y_x = merge_branches(fourier_mix(fx), att_x, gx, w_fourier[l], w_mla_o[l], b_gate[l], w_out[l])

        if not last:
            qc = mla_q(qlc, q_norm_g[l], w_qb[l], q_gain[l], None)
            att_c = context_attention(qc, kc, vc)
            y_c = merge_branches(fourier_mix(fc), att_c, gc, w_fourier[l], w_mla_o[l], b_gate[l], w_out[l])
            ctx = ctx + cg1 * y_c
            hc2 = modulate(rmsnorm(ctx, norm2_g[l]), csh2, csc2)
            ctx = ctx + cg2 * sq_relu_mlp(hc2, w_up[l], w_down[l])

        x = x + g1 * y_x
        h2 = modulate(rmsnorm(x, norm2_g[l]), sh2, sc2)
        x = x + g2 * sq_relu_mlp(h2, w_up[l], w_down[l])
    return x
```

```python
import os
import numpy as np
import ml_dtypes
from contextlib import ExitStack
import concourse.bass as bass
import concourse.mybir as mybir
from concourse.bass_utils import run_bass_kernel_spmd

F32 = mybir.dt.float32
BF16 = mybir.dt.bfloat16
F32R = mybir.dt.float32r
AF = mybir.ActivationFunctionType
ALU = mybir.AluOpType
AX = mybir.AxisListType

D = 1024
S = 4096
NOWN = 2048
DIN = 2976
EPS = 1e-6
SCALE = 96 ** -0.5


class _Op:
    __slots__ = ("eng", "fn", "deps", "sig", "is_dma", "semkey", "dma_cum", "needed")

    def __init__(self, eng, fn, is_dma=False, semkey=None):
        self.eng = eng
        self.fn = fn
        self.deps = []
        self.sig = None
        self.is_dma = is_dma
        self.semkey = semkey
        self.dma_cum = None
        self.needed = False


class Prog:
    ENGS = ("pe", "act", "dve", "pool", "sp")

    def __init__(self, nc):
        self.nc = nc
        self.ops = {e: [] for e in self.ENGS}
        self.last_w = {}
        self.readers = {}
        self.dma_counts = {}
        self.nbar = 0

    def add(self, eng, fn, reads=(), writes=(), dma=False, semkey=None):
        op = _Op(eng, fn, dma, semkey)
        deps = []
        for r in reads:
            w = self.last_w.get(r)
            if w is not None:
                deps.append(w)
            if isinstance(r, str) and r.startswith("ps") and eng != "pe":
                deps.extend(rd for rd in self.readers.get(r, ()) if rd.eng != eng)
        for w_ in writes:
            w = self.last_w.get(w_)
            if w is not None:
                deps.append(w)
            deps.extend(self.readers.get(w_, ()))
        seen = set()
        for d in deps:
            if id(d) in seen or d is op:
                continue
            seen.add(id(d))
            if (not d.is_dma) and d.eng == "pe" and eng == "pe" and not dma:
                continue
            op.deps.append(d)
            d.needed = True
        if dma:
            c = self.dma_counts.get(semkey, 0) + 16
            self.dma_counts[semkey] = c
            op.dma_cum = c
        for r in reads:
            self.readers.setdefault(r, []).append(op)
        for w_ in writes:
            self.last_w[w_] = op
            self.readers[w_] = []
        self.ops[eng].append(op)
        return op

    def barrier(self, fn):
        self.nbar += 1
        tag = "__bar%d" % self.nbar
        keys = [k for k in set(self.last_w) | set(self.readers) if not str(k).startswith("__bar")]
        self.add("dve", fn, writes=keys + [tag])
        for e in ("act", "pool", "sp", "pe"):
            self.add(e, None, reads=[tag])

    def emit(self, final_keys=()):
        nc = self.nc
        self.add("sp", None, reads=final_keys)
        with ExitStack() as es:
            esem = {e: es.enter_context(nc.semaphore("s_" + e)) for e in self.ENGS if e != "sp"}
            dsem = {k: es.enter_context(nc.semaphore("d_%d" % i)) for i, k in enumerate(self.dma_counts)}
            for e in self.ENGS:
                n = 0
                for op in self.ops[e]:
                    if op.is_dma:
                        continue
                    if op.needed:
                        n += 1
                        op.sig = n
            block = es.enter_context(nc.Block())

            def run(engh, e):
                waited = {}
                for op in self.ops[e]:
                    need = {}
                    for d in op.deps:
                        if d.is_dma:
                            s, v = dsem[d.semkey], d.dma_cum
                        else:
                            s, v = esem[d.eng], d.sig
                        k = id(s)
                        if k not in need or need[k][1] < v:
                            need[k] = (s, v)
                    for k, (s, v) in need.items():
                        if waited.get(k, 0) >= v:
                            continue
                        waited[k] = v
                        engh.wait_ge(s, v)
                    if op.fn is None:
                        continue
                    inst = op.fn(engh)
                    if op.is_dma:
                        inst.then_inc(dsem[op.semkey], 16)
                    elif op.needed:
                        inst.then_inc(esem[e], 1)

            @block.tensor
            def _(t):
                run(t, "pe")

            @block.scalar
            def _(t):
                run(t, "act")

            @block.vector
            def _(t):
                run(t, "dve")

            @block.gpsimd
            def _(t):
                run(t, "pool")

            @block.sync
            def _(t):
                run(t, "sp")


def build(debug=(), upto=None):
    nc = bass.Bass("TRN2", target_bir_lowering=False)

    def din(name, shape, dt=F32):
        return nc.dram_tensor(name, list(shape), dt, kind="ExternalInput").ap()

    xp = din("xp", [S, D])
    ctx = din("ctx", [256, D])
    cvec_d = din("cvec", [128, 16])
    ada_w = din("ada_w", [D, 6 * D])
    ada_b2 = din("ada_b2", [2, 6 * D])
    n1g_d = din("n1g", [128, 8])
    n2g_d = din("n2g", [128, 8])
    bgate_d = din("bgate", [128, 16])
    qng_d = din("qng", [128, 2])
    kvng_d = din("kvng", [128, 1])
    gk96_d = din("gk96", [128, 1])
    gq96_d = din("gq96", [128, 1])
    gkr_d = din("gkr", [128, 64])
    gqr_d = din("gqr", [128, 64])
    w_in = din("w_in", [D, DIN])
    w_fourier = din("w_fourier", [512, D])
    w_qb = din("w_qb", [256, 768])
    w_kvb = din("w_kvb", [128, 1024])
    w_mla_o = din("w_mla_o", [512, D])
    w_out = din("w_out", [D, D])
    w_up = din("w_up", [D, 4 * D])
    w_down = din("w_down", [4 * D, D])
    ident_d = din("ident", [128, 128], BF16)
    W1_d = din("W1", [128, 128], BF16)
    CS_d = din("CS", [128, 512], BF16)
    T2_d = din("T2", [128, 4096], BF16)
    cosK_d = din("cosK", [128, 34 * 32])
    sinK_d = din("sinK", [128, 34 * 32])
    sel_d = din("sel", [2, 130])
    out_d = nc.dram_tensor("out", [NOWN, D], F32, kind="ExternalOutput").ap()

    es = ExitStack()
    with es:
        def sb(name, shape, dt=F32):
            return es.enter_context(nc.sbuf_tensor("sb_" + name, list(shape), dt))

        ident = sb("ident", [128, 128], BF16)
        W1 = sb("W1", [128, 128], BF16)
        CS = sb("CS", [128, 512], BF16)
        cv = sb("cv", [128, 16])
        sg = sb("sg", [128, 16])
        scv = sb("scv", [128, 16], F32 if not os.environ.get("USEF32R") else F32R)
        modT = sb("modT", [128, 96])
        n1g = sb("n1g", [128, 8])
        n2g = sb("n2g", [128, 8])
        bgate = sb("bgate", [128, 16])
        qng = sb("qng", [128, 2])
        kvng = sb("kvng", [128, 1])
        gk96 = sb("gk96", [128, 1])
        gq96 = sb("gq96", [128, 1])
        gkr = sb("gkr", [128, 64])
        gqr = sb("gqr", [128, 64])
        G1 = sb("G1", [128, 8])
        G2 = sb("G2", [128, 8])
        cG1 = sb("cG1", [128, 8])
        g1b = sb("g1b", [128, 1024])
        g2b = sb("g2b", [128, 1024])
        sel = sb("sel", [2, 130])
        ones_f = sb("ones_f", [128, 128])
        mhalf = sb("mhalf", [128, 8])
        st = sb("st", [128, 256])
        cosK = sb("cosK", [128, 34 * 32])
        sinK = sb("sinK", [128, 34 * 32])
        cosQ = sb("cosQ", [128, 16 * 32])
        sinQ = sb("sinQ", [128, 16 * 32])
        ssn = sb("ssn", [128, 16])
        rk = sb("rk", [128, 16])
        ARW = 41472
        arena = sb("arena", [128, ARW])
        psall = es.enter_context(nc.psum_tensor("psall", [128, 4096], F32))

        def PS(i):
            return psall[:, i * 512:(i + 1) * 512]

        def PSB(i):
            return psall[:, i * 512:(i + 1) * 512].bitcast(BF16)

        def pk(i):
            return "ps%d" % i

        SSX, RSX, SSKV, RSKV, SSR, SSQ, RSQ, SS2, RS2 = 0, 34, 68, 102, 136, 170, 186, 202, 218

        class Arena:
            def __init__(self, off=0):
                self.off = off

            def f32(self, nwords):
                assert nwords % 2 == 0 and self.off % 2 == 0
                a = arena[:, self.off:self.off + nwords]
                self.off += nwords
                assert self.off <= ARW, self.off
                return a

            def bf(self, nelem):
                assert nelem % 2 == 0
                return self.f32(nelem // 2).bitcast(BF16)

        P = Prog(nc)
        A = P.add

        def dma_sp(out, in_, key, reads=()):
            A("sp", lambda e: e.dma_start(out=out, in_=in_), reads=reads, writes=[key], dma=True, semkey=key)

        def dma_pool(out, in_, key, reads=()):
            A("pool", lambda e: e.dma_start(out=out, in_=in_), reads=reads, writes=[key], dma=True, semkey=key)

        def act(out, in_, func, r, w, **kw):
            A("act", lambda e: e.activation(out=out, in_=in_, func=func, **kw), reads=r, writes=w)

        def tsc(eng, out, in0, s1, s2, op0, op1, r, w):
            if op1 is None:
                A(eng, lambda e: e.tensor_scalar(out=out, in0=in0, scalar1=s1, scalar2=None, op0=op0), reads=r, writes=w)
            else:
                A(eng, lambda e: e.tensor_scalar(out=out, in0=in0, scalar1=s1, scalar2=s2, op0=op0, op1=op1), reads=r, writes=w)

        def tt(eng, out, in0, in1, op, r, w):
            A(eng, lambda e: e.tensor_tensor(out=out, in0=in0, in1=in1, op=op), reads=r, writes=w)

        def cp(eng, out, in_, r, w):
            A(eng, lambda e: e.tensor_copy(out=out, in_=in_), reads=r, writes=w)

        def mm(out, lhsT, rhs, start, stop, r, w):
            A("pe", lambda e: e.matmul(out, lhsT=lhsT, rhs=rhs, start=start, stop=stop), reads=r, writes=w)

        def tr(out, in_, r, w):
            A("pe", lambda e: e.transpose(out=out, in_=in_, identity=ident[:]), reads=list(r) + ["ident"], writes=w)

        def rsqrt_col(col_ss, col_rs, mul, keys_r, key_w, n=1):
            tsc("dve", st[:, col_rs:col_rs + n], st[:, col_ss:col_ss + n], mul, EPS, ALU.mult, ALU.add, keys_r, [key_w])
            tt("pool", st[:, col_rs:col_rs + n], st[:, col_rs:col_rs + n], mhalf[:, 0:n], ALU.pow, [key_w, "mhalf"], [key_w])

        def barrier():
            if os.environ.get("NOBAR"):
                return
            P.barrier(lambda e: e.memset(ssn[:, 15:16], 0.0))

        DBG = {}
        fin = []

        def finish():
            for name in debug:
                ap_, shp, dt_ = DBG[name]
                dd = nc.dram_tensor("dbg_" + name, list(shp), dt_, kind="ExternalOutput").ap()
                dma_sp(dd, ap_, "dbg_" + name, reads=[k for k in P.last_w.keys() if not str(k).startswith("dbg_")])
                fin.append("dbg_" + name)
            P.emit(final_keys=fin)

        for (t_, d_, k_) in [(ident, ident_d, "ident"), (W1, W1_d, "W1"), (CS, CS_d, "CS"),
                             (cv, cvec_d, "cv"), (n1g, n1g_d, "n1g"), (n2g, n2g_d, "n2g"), (bgate, bgate_d, "bgate"),
                             (qng, qng_d, "qng"), (kvng, kvng_d, "kvng"), (gk96, gk96_d, "gk96"), (gq96, gq96_d, "gq96"),
                             (gkr, gkr_d, "gkr"), (gqr, gqr_d, "gqr"), (sel, sel_d, "sel"),
                             (cosK, cosK_d, "cosK"), (sinK, sinK_d, "sinK")]:
            dma_sp(t_[:], d_, k_)
        A("dve", lambda e: e.memset(st[:], 0.0), writes=["st_init"])
        A("dve", lambda e: e.memset(mhalf[:], -0.5), writes=["mhalf"])
        A("dve", lambda e: e.memset(ones_f[:], 1.0), writes=["ones_f"])
        A("dve", lambda e: e.memset(ssn[:], 0.0), writes=["ssn"])

        act(sg[:], cv[:], AF.Sigmoid, ["cv"], ["sg"])
        tt("dve", scv[:], cv[:], sg[:], ALU.mult, ["cv", "sg"], ["scv"])
        scv3 = scv[:].rearrange("p (c r) -> p c r", r=2)

        ar = Arena(0)
        modrow = ar.f32(6144)
        abias = ar.f32(6144)
        NSTG = 4
        stage = [ar.f32(4096).rearrange("p (c n) -> p c n", c=8) for _ in range(NSTG)]
        dma_sp(abias[0:2, :], ada_b2, "abias")
        adaw3 = ada_w.rearrange("(c p) n -> p c n", p=128)
        for blk in range(12):
            s_ = blk % NSTG
            dma_sp(stage[s_], adaw3[:, :, blk * 512:(blk + 1) * 512], "stage%d" % s_)
            pi = blk % 2
            for c in range(8):
                if not os.environ.get("USEF32R"):
                    mm(PS(pi)[0:2, :], scv3[:, c, :], stage[s_][:, c, :], c == 0, c == 7, ["scv", "stage%d" % s_], [pk(pi)])
                else:
                    mm(PS(pi)[0:2, :], scv3[:, c, :], stage[s_][:, c, :].bitcast(F32R), c == 0, c == 7,
                       ["scv", "stage%d" % s_], [pk(pi)])
            tt("dve", modrow[0:2, blk * 512:(blk + 1) * 512], PS(pi)[0:2, :], abias[0:2, blk * 512:(blk + 1) * 512], ALU.add,
               [pk(pi), "abias"], ["modrow%d" % blk])
        mrk = ["modrow%d" % b_ for b_ in range(12)]
        for ch in range(48):
            mm(PS(2)[:, 2 * ch:2 * ch + 2], modrow[0:2, ch * 128:(ch + 1) * 128], sel[0:2, 0:2], True, True,
               mrk + ["sel"], [pk(2)])
        cp("dve", modT[:], PS(2)[:, 0:96], [pk(2)], ["modT"])
        modT3 = modT[:].rearrange("p (c r) -> p c r", r=2)
        for half in range(2):
            mm(PS(3)[:, :], sel[0:2, 2:130], modrow[0:2, 2048 + half * 512:2048 + (half + 1) * 512], True, True, mrk + ["sel"], [pk(3)])
            cp("dve", g1b[:, half * 512:(half + 1) * 512], PS(3), [pk(3)], ["g1b"])
            mm(PS(4)[:, :], sel[0:2, 2:130], modrow[0:2, 5120 + half * 512:5120 + (half + 1) * 512], True, True, mrk + ["sel"], [pk(4)])
            cp("dve", g2b[:, half * 512:(half + 1) * 512], PS(4), [pk(4)], ["g2b"])

        def stt(out, in0, scalar, in1, op0, op1, r, w):
            A("dve", lambda e: e.scalar_tensor_tensor(out=out, in0=in0, scalar=scalar, in1=in1, op0=op0, op1=op1), reads=r, writes=w)

        stt(G1[:], modT3[:, 8:16, 0], 1.0, n1g[:], ALU.add, ALU.mult, ["modT", "n1g"], ["G1"])
        stt(cG1[:], modT3[:, 8:16, 1], 1.0, n1g[:], ALU.add, ALU.mult, ["modT", "n1g"], ["cG1"])
        stt(G2[:], modT3[:, 32:40, 0], 1.0, n2g[:], ALU.add, ALU.mult, ["modT", "n2g"], ["G2"])

        cosK3 = cosK[:].rearrange("p (t f) -> p t f", f=32)
        sinK3 = sinK[:].rearrange("p (t f) -> p t f", f=32)
        cosQ3 = cosQ[:].rearrange("p (t f) -> p t f", f=32)
        sinQ3 = sinQ[:].rearrange("p (t f) -> p t f", f=32)
        tt("dve", cosQ3, cosK3[:, 0:16, :], gqr[:, 0:32].unsqueeze(1).broadcast_to([128, 16, 32]), ALU.mult, ["cosK", "gqr"], ["cosQ"])
        tt("dve", sinQ3, sinK3[:, 0:16, :], gqr[:, 32:64].unsqueeze(1).broadcast_to([128, 16, 32]), ALU.mult, ["sinK", "gqr"], ["sinQ"])
        tt("dve", cosK3, cosK3, gkr[:, 0:32].unsqueeze(1).broadcast_to([128, 34, 32]), ALU.mult, ["cosK", "gkr", "cosQ"], ["cosK"])
        tt("dve", sinK3, sinK3, gkr[:, 32:64].unsqueeze(1).broadcast_to([128, 34, 32]), ALU.mult, ["sinK", "gkr", "sinQ"], ["sinK"])

        DBG.update({"modT": (modT[:], [128, 96], F32), "g1b": (g1b[:], [128, 1024], F32), "g2b": (g2b[:], [128, 1024], F32),
                    "G1": (G1[:], [128, 8], F32), "cosK": (cosK[:], [128, 34 * 32], F32), "sinQ": (sinQ[:], [128, 512], F32)})
        if upto == "P0":
            finish()
            return nc
        barrier()

        ar = Arena(0)
        kvnT = ar.bf(34 * 128)
        krr = ar.f32(34 * 32).rearrange("p (t f) -> p t f", f=32)
        qnT = ar.bf(2 * 2048).rearrange("p (c n) -> p c n", c=2)
        OFF_DEAD3 = ar.off
        fmixT = ar.bf(4 * 2048).rearrange("p (g n) -> p g n", g=4)
        hxTown = ar.bf(8 * 2048).rearrange("p (c n) -> p c n", c=8)
        wkvb = ar.bf(1024)
        wqb = ar.bf(2 * 768).rearrange("p (c n) -> p c n", c=2)
        OFF_P2 = ar.off
        winA = ar.bf(8 * 928).rearrange("p (c n) -> p c n", c=8)
        xt = [ar.f32(1024) for _ in range(2)]
        junk = ar.bf(1024)
        xn = [ar.bf(1024) for _ in range(2)]
        hxT = [ar.bf(1024).rearrange("p (c n) -> p c n", c=8) for _ in range(2)]
        Ftok = [ar.bf(512) for _ in range(2)]
        Hall = ar.bf(4 * 4096).rearrange("p (g r n) -> p g r n", g=4, r=64)
        Zg = [ar.bf(4096).rearrange("p (a m) -> p a m", a=16) for _ in range(1)]
        kvn = [ar.bf(128) for _ in range(2)]
        qn = [ar.bf(256) for _ in range(2)]
        ktmp = [ar.f32(32) for _ in range(4)]
        T2 = ar.bf(4096)

        dma_sp(T2, T2_d, "T2")
        w_in3 = w_in.rearrange("(c p) n -> p c n", p=128)
        if not os.environ.get("NOPOOLW"):
            dma_pool(winA, w_in3[:, :, 0:928], "winA")
            dma_pool(wkvb, w_kvb, "wkvb")
            dma_pool(wqb, w_qb.rearrange("(c p) n -> p c n", p=128), "wqb")

        def tinfo(t):
            return t < 16, t >= 32, t % 2

        def hkeys_of(t):
            own, isctx, s2 = tinfo(t)
            return [("hxTown_%d_%d" % (t, c)) if own else ("hxT%d_%d" % (s2, c)) for c in range(8)]

        def S1(t):
            own, isctx, s2 = tinfo(t)
            kxt = "xt%d" % s2
            src = ctx[(t - 32) * 128:(t - 31) * 128, :] if isctx else xp[t * 128:(t + 1) * 128, :]
            dma_sp(xt[s2], src, kxt)
            act(junk, xt[s2], AF.Square, [kxt, "st_init"], ["ssx%d" % t], accum_out=st[:, SSX + t:SSX + t + 1])
            rsqrt_col(SSX + t, RSX + t, 1.0 / D, ["ssx%d" % t], "rsx%d" % t)
            tsc("dve", xn[s2], xt[s2], st[:, RSX + t:RSX + t + 1], None, ALU.mult, None, [kxt, "rsx%d" % t], ["xn%d" % s2])

        def S2(t):
            own, isctx, s2 = tinfo(t)
            tp = s2
            for c in range(8):
                tr(PSB(tp)[:, c * 128:(c + 1) * 128], xn[s2][:, c * 128:(c + 1) * 128], ["xn%d" % s2], [pk(tp)])
            Gm = cG1 if isctx else G1
            rr = 1 if isctx else 0
            hk = hkeys_of(t)
            for c in range(8):
                dest = hxTown[:, c, t * 128:(t + 1) * 128] if own else hxT[s2][:, c, :]
                src_ = PSB(tp)[:, c * 128:(c + 1) * 128]
                if c < 4:
                    act(dest, src_, AF.Identity, [pk(tp), "G1", "cG1", "modT"], [hk[c]],
                        scale=Gm[:, c:c + 1], bias=modT3[:, c, rr:rr + 1])
                else:
                    tsc("dve", dest, src_, Gm[:, c:c + 1], modT3[:, c, rr:rr + 1], ALU.mult, ALU.add,
                        [pk(tp), "G1", "cG1", "modT"], [hk[c]])

        def S3(t):
            own, isctx, s2 = tinfo(t)
            hk = hkeys_of(t)

            def hx(c):
                return hxTown[:, c, t * 128:(t + 1) * 128] if own else hxT[s2][:, c, :]
            bA0 = 2 + 2 * s2
            bA1 = 3 + 2 * s2
            if not isctx:
                for c in range(8):
                    mm(PS(bA0), hx(c), winA[:, c, 0:512], c == 0, c == 7, hk + ["winA"], [pk(bA0)])
            c0 = 512 if own else 768
            for c in range(8):
                mm(PS(bA1)[:, c0 - 512:416], hx(c), winA[:, c, c0:928], c == 0, c == 7, hk + ["winA"], [pk(bA1)])
            if not isctx:
                kf = "Ftok%d" % s2
                act(Ftok[s2], PS(bA0), AF.Copy, [pk(bA0)], [kf])
                for g in range(4):
                    mm(PS(6)[:, g * 128:(g + 1) * 128], Ftok[s2][:, g * 128:(g + 1) * 128], W1[:, :], True, True, [kf, "W1"], [pk(6)])
            act(junk[:, 0:128], PS(bA1)[:, 256:384], AF.Square, [pk(bA1), "st_init"], ["sskv%d" % t],
                accum_out=st[:, SSKV + t:SSKV + t + 1])
            act(junk[:, 128:160], PS(bA1)[:, 384:416], AF.Square, [pk(bA1), "st_init"], ["ssr%d" % t],
                accum_out=st[:, SSR + t:SSR + t + 1])
            if own:
                act(junk[:, 256:512], PS(bA1)[:, 0:256], AF.Square, [pk(bA1), "st_init"], ["ssq%d" % t],
                    accum_out=st[:, SSQ + t:SSQ + t + 1])
            rsqrt_col(SSKV + t, RSKV + t, 1.0 / 128, ["sskv%d" % t], "rskv%d" % t)
            if own:
                rsqrt_col(SSQ + t, RSQ + t, 1.0 / 256, ["ssq%d" % t], "rsq%d" % t)

        def S4(t):
            own, isctx, s2 = tinfo(t)
            bA1 = 3 + 2 * s2
            if not isctx:
                sidx = t // 16
                j = t % 16
                for q in range(2):
                    n2 = 32 * sidx + 2 * j + q
                    cp("dve", Hall[:, :, :, n2], PS(6).rearrange("p (g q r) -> p g q r", g=4, q=2)[:, :, q, :],
                       [pk(6)], ["Hall_%d_%d" % (t, q)])
            tsc("dve", kvn[s2], PS(bA1)[:, 256:384], st[:, RSKV + t:RSKV + t + 1], None, ALU.mult, None, [pk(bA1), "rskv%d" % t],
                ["kvn%d" % s2])
            kr4 = PS(bA1)[:, 384:416].rearrange("p (a h f) -> p a h f", a=2, h=2)
            sn4 = sinK3[:, t, :].rearrange("p (a h f) -> p a h f", a=2, h=2)
            u1 = ktmp[s2]
            u2 = ktmp[2 + s2]
            u24 = u2.rearrange("p (a h f) -> p a h f", a=2, h=2)
            tt("dve", u1, PS(bA1)[:, 384:416], cosK3[:, t, :], ALU.mult, [pk(bA1), "cosK"], ["u1_%d" % s2])
            tt("dve", u24[:, :, 0, :], kr4[:, :, 1, :], sn4[:, :, 0, :], ALU.mult, [pk(bA1), "sinK"], ["u2a_%d" % s2])
            tt("dve", u24[:, :, 1, :], kr4[:, :, 0, :], sn4[:, :, 1, :], ALU.mult, [pk(bA1), "sinK"], ["u2b_%d" % s2])
            tt("pool", krr[:, t, :], u1, u2, ALU.add, ["u1_%d" % s2, "u2a_%d" % s2, "u2b_%d" % s2], ["krr%d" % t])
            if own:
                tsc("dve", qn[s2], PS(bA1)[:, 0:256], st[:, RSQ + t:RSQ + t + 1], None, ALU.mult, None, [pk(bA1), "rsq%d" % t],
                    ["qn%d" % s2])

        def S5(t):
            own, isctx, s2 = tinfo(t)
            tr(PSB(7)[:, 0:128], kvn[s2], ["kvn%d" % s2], [pk(7)])
            if own:
                for c in range(2):
                    tr(PSB(7)[:, 128 + c * 128:256 + c * 128], qn[s2][:, c * 128:(c + 1) * 128], ["qn%d" % s2], [pk(7)])
            act(kvnT[:, t * 128:(t + 1) * 128], PSB(7)[:, 0:128], AF.Identity, [pk(7), "kvng"], ["kvnT%d" % t], scale=kvng[:, 0:1])
            if own:
                for c in range(2):
                    act(qnT[:, c, t * 128:(t + 1) * 128], PSB(7)[:, 128 + c * 128:256 + c * 128], AF.Identity, [pk(7), "qng"],
                        ["qnT%d_%d" % (t, c)], scale=qng[:, c:c + 1])

        stages = [S1, S2, S3, S4, S5]
        NT = 34
        for it_ in range(NT + len(stages) - 1):
            for si in reversed(range(len(stages))):
                t = it_ - si
                if 0 <= t < NT:
                    stages[si](t)

        hallk = ["Hall_%d_%d" % (t, q) for t in range(32) for q in range(2)]
        T23 = T2.rearrange("p (a k r n) -> p a k r n", a=16, k=2, r=2)
        for g in range(4):
            zs = 0
            kz = "Zg%d" % zs
            for pair in range(16):
                pz = pair % 2
                mm(PS(pz)[:, 0:256], Hall[:, g, 2 * pair:2 * pair + 2, :].rearrange("p a n -> p (a n)"), CS[:, 0:256], True, False,
                   hallk + ["CS"], [pk(pz)])
                mm(PS(pz)[:, 0:256], Hall[:, g, 32 + 2 * pair:32 + 2 * pair + 2, :].rearrange("p a n -> p (a n)"), CS[:, 256:512], False, True,
                   hallk + ["CS"], [pk(pz)])
                if pair % 2 == 0:
                    act(Zg[zs][:, pair, :], PS(pz)[:, 0:256], AF.Copy, [pk(pz)], [kz + "_%d" % pair])
                else:
                    cp("dve", Zg[zs][:, pair, :], PS(pz)[:, 0:256], [pk(pz)], [kz + "_%d" % pair])
            for oct_ in range(4):
                py = 2 + oct_ % 2
                for i8 in range(8):
                    k1l = oct_ * 8 + i8
                    pair, kk = k1l // 2, k1l % 2
                    mm(PS(py)[:, i8 * 64:(i8 + 1) * 64], Zg[zs][:, pair, 0:128], T23[:, pair, kk, 0, :], True, False,
                       [kz + "_%d" % pair, "T2"], [pk(py)])
                    mm(PS(py)[:, i8 * 64:(i8 + 1) * 64], Zg[zs][:, pair, 128:256], T23[:, pair, kk, 1, :], False, True,
                       [kz + "_%d" % pair, "T2"], [pk(py)])
                if oct_ % 2 == 0:
                    act(fmixT[:, g, oct_ * 512:(oct_ + 1) * 512], PS(py), AF.Copy, [pk(py)], ["fmixT_%d_%d" % (g, oct_)])
                else:
                    cp("dve", fmixT[:, g, oct_ * 512:(oct_ + 1) * 512], PS(py), [pk(py)], ["fmixT_%d_%d" % (g, oct_)])

        DBG.update({"st": (st[:], [128, 256], F32), "kvnT": (kvnT, [128, 4352], BF16), "krr": (arena[:, 2176:2176 + 1088], [128, 1088], F32),
                    "qnT": (arena[:, 3264:3264 + 2048].bitcast(BF16), [128, 4096], BF16),
                    "fmixT": (arena[:, 5312:5312 + 4096].bitcast(BF16), [128, 8192], BF16),
                    "hxTown": (arena[:, 9408:9408 + 8192].bitcast(BF16), [128, 16384], BF16)})
        if upto == "P1":
            finish()
            return nc
        barrier()

        ar = Arena(OFF_P2)
        attT = ar.bf(8 * 2048)[:, 0:4 * 2048].rearrange("p (h n) -> p h n", h=4)
        OFF_P3A = ar.off
        KT = [ar.bf(4352) for _ in range(2)]
        QT = [ar.bf(2048) for _ in range(2)]
        Vh = [ar.bf(34 * 128).rearrange("p (t f) -> p t f", f=128) for _ in range(2)]
        pT = [ar.bf(1024) for _ in range(2)]
        sqs = ar.f32(384)
        khb = [ar.bf(4 * 96).rearrange("p (i f) -> p i f", f=96) for _ in range(2)]
        qtmp = [ar.f32(128) for _ in range(2)]
        Rb = ar.f32(512)
        rcp = ar.f32(512)

        for s_ in range(2):
            A("pool", lambda e, s_=s_: e.memset(KT[s_], 0.0), writes=["KT%d" % s_])
            A("pool", lambda e, s_=s_: e.memset(QT[s_], 0.0), writes=["QT%d" % s_])
            A("dve", lambda e, s_=s_: e.memset(Vh[s_], 0.0), writes=["Vones%d" % s_])
            oc = 64 if s_ == 0 else 0
            A("dve", lambda e, s_=s_, oc=oc: e.memset(Vh[s_][:, :, oc:oc + 1], 1.0), writes=["Vones%d" % s_])

        def gen_kv_mm(h, grp, slot):
            tiles = list(range(4 * grp, min(4 * grp + 4, 34)))
            for i, kt in enumerate(tiles):
                mm(PS(6)[:, i * 128:(i + 1) * 128], kvnT[:, kt * 128:(kt + 1) * 128], wkvb[:, h * 128:(h + 1) * 128], True, True,
                   ["kvnT%d" % kt, "wkvb"], [pk(6)])
            n = len(tiles)
            t0 = tiles[0]
            kv3 = PS(6)[:, 0:n * 128].rearrange("p (i f) -> p i f", f=128)
            sq3 = sqs[:, 0:n * 64].rearrange("p (i f) -> p i f", f=64)
            act(sq3, kv3[:, :, 0:64], AF.Square, [pk(6)], ["sqs"])
            vc = 0 if slot == 0 else 64
            cp("dve", Vh[slot][:, t0:t0 + n, vc:vc + 64], kv3[:, :, 64:128], [pk(6), "Vones%d" % slot], ["V%d" % slot])
            A("dve", lambda e: e.tensor_reduce(out=ssn[:, 0:n], in_=sq3, axis=AX.X, op=ALU.add), reads=["sqs"], writes=["ssn"])
            tt("dve", ssn[:, 0:n], ssn[:, 0:n], st[:, SSR + t0:SSR + t0 + n], ALU.add, ["ssn"] + ["ssr%d" % k for k in tiles], ["ssn"])
            tsc("dve", rk[:, 0:n], ssn[:, 0:n], 1.0 / 96, EPS, ALU.mult, ALU.add, ["ssn"], ["rk"])
            tt("pool", rk[:, 0:n], rk[:, 0:n], mhalf[:, 0:n], ALU.pow, ["rk", "mhalf"], ["rk"])
            kb = khb[grp % 2]
            kkb = "khb%d" % (grp % 2)
            tt("dve", kb[:, 0:n, 0:64], kv3[:, :, 0:64], rk[:, 0:n].unsqueeze(2).broadcast_to([128, n, 64]), ALU.mult, [pk(6), "rk"], [kkb])
            tt("dve", kb[:, 0:n, 64:96], krr[:, t0:t0 + n, :], rk[:, 0:n].unsqueeze(2).broadcast_to([128, n, 32]), ALU.mult,
               ["krr%d" % k for k in tiles] + ["rk"], [kkb])
            return tiles

        def gen_kv_tr(h, grp, slot, tiles):
            n = len(tiles)
            t0 = tiles[0]
            kb = khb[grp % 2]
            kkb = "khb%d" % (grp % 2)
            for i in range(n):
                tr(PSB(7)[0:96, i * 128:(i + 1) * 128], kb[:, i, :], [kkb], [pk(7)])
            tsc("dve", KT[slot][0:96, t0 * 128:(t0 + n) * 128], PSB(7)[0:96, 0:n * 128], gk96[0:96, 0:1], None, ALU.mult, None,
                [pk(7), "gk96"], ["KT%d" % slot])

        def gen_q_mm(h, grp9, slot):
            grp = grp9 - 9
            for i in range(4):
                j = 4 * grp + i
                for c in range(2):
                    mm(PS(6)[:, i * 96:(i + 1) * 96], qnT[:, c, j * 128:(j + 1) * 128], wqb[:, c, h * 96:(h + 1) * 96], c == 0, c == 1,
                       ["qnT%d_%d" % (j, c), "wqb"], [pk(6)])
            q3 = PS(6)[:, 0:384].rearrange("p (i f) -> p i f", f=96)
            sq3 = sqs[:, 0:384].rearrange("p (i f) -> p i f", f=96)
            act(sq3, q3, AF.Square, [pk(6)], ["sqs"])
            A("dve", lambda e: e.tensor_reduce(out=ssn[:, 0:4], in_=sq3, axis=AX.X, op=ALU.add), reads=["sqs"], writes=["ssn"])
            tsc("dve", rk[:, 0:4], ssn[:, 0:4], 1.0 / 96, EPS, ALU.mult, ALU.add, ["ssn"], ["rk"])
            tt("pool", rk[:, 0:4], rk[:, 0:4], mhalf[:, 0:4], ALU.pow, ["rk", "mhalf"], ["rk"])
            kb = khb[grp9 % 2]
            kkb = "khb%d" % (grp9 % 2)
            tt("dve", kb[:, :, 0:64], q3[:, :, 0:64], rk[:, 0:4].unsqueeze(2).broadcast_to([128, 4, 64]), ALU.mult, [pk(6), "rk"], [kkb])
            j0 = 4 * grp
            qr = q3[:, :, 64:96]
            qr5 = qr.rearrange("p i (a h f) -> p i a h f", a=2, h=2)
            u1 = qtmp[0].rearrange("p (i f) -> p i f", f=32)
            u2 = qtmp[1].rearrange("p (i f) -> p i f", f=32)
            u25 = qtmp[1].rearrange("p (i a h f) -> p i a h f", i=4, a=2, h=2)
            sn5 = sinQ3[:, j0:j0 + 4, :].rearrange("p i (a h f) -> p i a h f", a=2, h=2)
            tt("dve", u1, qr, cosQ3[:, j0:j0 + 4, :], ALU.mult, [pk(6), "cosQ"], ["qu1"])
            for hh in range(2):
                for a_ in range(2):
                    tt("dve", u25[:, :, a_, hh, :], qr5[:, :, a_, 1 - hh, :], sn5[:, :, a_, hh, :], ALU.mult, [pk(6), "sinQ"],
                       ["qu2_%d%d" % (hh, a_)])
            tt("pool", u1, u1, u2, ALU.add, ["qu1"] + ["qu2_%d%d" % (hh, a_) for hh in range(2) for a_ in range(2)], ["qu1"])
            tt("dve", kb[:, :, 64:96], u1, rk[:, 0:4].unsqueeze(2).broadcast_to([128, 4, 32]), ALU.mult, ["qu1", "rk"], [kkb])

        def gen_q_tr(h, grp9, slot):
            grp = grp9 - 9
            kb = khb[grp9 % 2]
            kkb = "khb%d" % (grp9 % 2)
            for i in range(4):
                tr(PSB(7)[0:96, i * 128:(i + 1) * 128], kb[:, i, :], [kkb], [pk(7)])
            tsc("dve", QT[slot][0:96, grp * 512:(grp + 1) * 512], PSB(7)[0:96, 0:512], gq96[0:96, 0:1], None, ALU.mult, None,
                [pk(7), "gq96"], ["QT%d" % slot])

        def gen_tasks(h, slot):
            mms, trs = [], []
            for grp in range(9):
                box = {}
                mms.append(lambda grp=grp, box=box: box.__setitem__("t", gen_kv_mm(h, grp, slot)))
                trs.append(lambda grp=grp, box=box: gen_kv_tr(h, grp, slot, box["t"]))
            for grp in range(4):
                mms.append(lambda grp=grp: gen_q_mm(h, grp + 9, slot))
                trs.append(lambda grp=grp: gen_q_tr(h, grp + 9, slot))
            tasks = [mms[0]]
            for i in range(1, len(mms)):
                tasks.append(mms[i])
                tasks.append(trs[i - 1])
            tasks.append(trs[-1])
            return tasks

        for tk in gen_tasks(0, 0):
            tk()

        NPR = 17
        pending = []

        def make_norm(h, qb, ob):
            par = h % 2
            lo = 64 * par
            rs = 64 if par == 0 else 0

            def fn():
                A("dve", lambda e: e.reciprocal(out=rcp[0:1, :], in_=PS(ob)[rs:rs + 1, :]), reads=[pk(ob)], writes=["rcp"])
                mm(PS(7)[:, :], ones_f[0:1, 0:128], rcp[0:1, :], True, True, ["ones_f", "rcp"], [pk(7)])
                cp("dve", Rb[lo:lo + 64, :], PS(7)[lo:lo + 64, :], [pk(7)], ["Rb"])
                tt("dve", attT[lo:lo + 64, h // 2, qb * 512:(qb + 1) * 512], PS(ob)[lo:lo + 64, :], Rb[lo:lo + 64, :], ALU.mult,
                   [pk(ob), "Rb"], ["attT_%d_%d" % (h, qb)])
            return fn

        for h in range(8):
            slot = h % 2
            nxt = gen_tasks(h + 1, 1 - slot) if h < 7 else []
            total_steps = 4 * (NPR + 2)
            stepno = 0
            ti = 0
            for qb in range(4):
                ob = 4 + qb % 2
                for step in range(NPR + 2):
                    if step < NPR:
                        sb0 = 2 * (step % 2)
                        for u in range(2):
                            kt = 2 * step + u
                            mm(PS(sb0 + u), KT[slot][:, kt * 128:(kt + 1) * 128], QT[slot][:, qb * 512:(qb + 1) * 512], True, True,
                               ["KT%d" % slot, "QT%d" % slot], [pk(sb0), pk(sb0 + 1)] if u == 1 else [pk(sb0)])
                    if 1 <= step <= NPR:
                        pr = step - 1
                        sb0 = 2 * (pr % 2)
                        act(pT[pr % 2], psall[:, sb0 * 512:(sb0 + 2) * 512], AF.Exp, [pk(sb0), pk(sb0 + 1)], ["pT%d" % (pr % 2)], scale=SCALE)
                    if step >= 2:
                        pr = step - 2
                        for u in range(2):
                            kt = 2 * pr + u
                            mm(PS(ob)[:, :], Vh[slot][:, kt, :], pT[pr % 2][:, u * 512:(u + 1) * 512], kt == 0, kt == 2 * NPR - 1,
                               ["V%d" % slot, "Vones%d" % slot, "pT%d" % (pr % 2)], [pk(ob)])
                    if step == 4 and pending:
                        pending.pop(0)()
                    stepno += 1
                    if nxt and ti < len(nxt) and stepno * len(nxt) >= (ti + 1) * total_steps * 0.9:
                        nxt[ti]()
                        ti += 1
                pending.append(make_norm(h, qb, ob))
            while ti < len(nxt):
                nxt[ti]()
                ti += 1
        while pending:
            pending.pop(0)()

        DBG.update({"attT": (arena[:, OFF_P2:OFF_P2 + 8192].bitcast(BF16), [128, 16384], BF16),
                    "KT1": (KT[1], [128, 4352], BF16), "QT1": (QT[1], [128, 2048], BF16)})
        if upto == "P2":
            finish()
            return nc
        barrier()

        ar = Arena(0)
        t12 = [ar.f32(1024) for _ in range(2)]
        sfa = [ar.bf(1024) for _ in range(2)]
        wf = [ar.bf(4 * 128).rearrange("p (g n) -> p g n", g=4) for _ in range(2)]
        wm = [ar.bf(8 * 128)[:, 0:512].rearrange("p (h n) -> p h n", h=4) for _ in range(2)]
        assert ar.off <= OFF_DEAD3, (ar.off, OFF_DEAD3)
        ar = Arena(OFF_P3A)
        yT = ar.bf(8 * 2048).rearrange("p (c n) -> p c n", c=8)
        OFF_YEND = ar.off
        wg = [ar.bf(8 * 2 * 128).rearrange("p (c f n) -> p c f n", c=8, f=2) for _ in range(2)]

        w_f3 = w_fourier.rearrange("(g p) n -> p g n", p=128)
        w_m3 = w_mla_o.rearrange("(h p) n -> p h n", p=128)

        def load_dc(dc):
            s_ = dc % 2
            for fa in range(2):
                c0_ = 928 + fa * 1024 + dc * 128
                dma_pool(wg[s_][:, :, fa, :], w_in3[:, :, c0_:c0_ + 128], "wg%d_%d" % (s_, fa))
            dma_pool(wf[s_], w_f3[:, :, dc * 128:(dc + 1) * 128], "wf%d" % s_)
            dma_pool(wm[s_], w_m3[:, :, dc * 128:(dc + 1) * 128], "wm%d" % s_)

        load_dc(0)
        load_dc(1)
        it = 0
        for dc in range(8):
            s_ = dc % 2
            for blk in range(4):
                bs = 4 * (it % 2)
                b2 = it % 2
                it += 1
                tok = slice(blk * 512, (blk + 1) * 512)
                hk = ["hxTown_%d_%d" % (t_, c_) for t_ in range(4 * blk, 4 * blk + 4) for c_ in range(8)]
                for fa in range(2):
                    for c in range(8):
                        mm(PS(bs + fa), wg[s_][:, c, fa, :], hxTown[:, c, tok], c == 0, c == 7,
                           ["wg%d_%d" % (s_, fa)] + hk, [pk(bs + fa)])
                fk = ["fmixT_%d_%d" % (g_, blk) for g_ in range(4)]
                for g in range(4):
                    mm(PS(bs + 2), wf[s_][:, g, :], fmixT[:, g, tok], g == 0, g == 3, ["wf%d" % s_] + fk, [pk(bs + 2)])
                ak = ["attT_%d_%d" % (h_, blk) for h_ in range(8)]
                for hp in range(4):
                    mm(PS(bs + 3), wm[s_][:, hp, :], attT[:, hp, tok], hp == 0, hp == 3, ["wm%d" % s_] + ak, [pk(bs + 3)])
                sf = sfa[b2][:, 0:512]
                sa = sfa[b2][:, 512:1024]
                t1 = t12[b2][:, 0:512]
                t2 = t12[b2][:, 512:1024]
                act(sf, PS(bs), AF.Sigmoid, [pk(bs), "bgate"], ["sf%d" % b2], bias=bgate[:, dc:dc + 1])
                act(sa, PS(bs + 1), AF.Sigmoid, [pk(bs + 1), "bgate"], ["sa%d" % b2], bias=bgate[:, 8 + dc:9 + dc])
                tt("dve", t1, PS(bs + 2), sf, ALU.mult, [pk(bs + 2), "sf%d" % b2], ["t1_%d" % b2])
                tt("dve", t2, PS(bs + 3), sa, ALU.mult, [pk(bs + 3), "sa%d" % b2], ["t2_%d" % b2])
                tt("pool", yT[:, dc, tok], t1, t2, ALU.add, ["t1_%d" % b2, "t2_%d" % b2], ["yT_%d_%d" % (dc, blk)])
            if dc + 2 < 8:
                load_dc(dc + 2)

        DBG.update({"yT": (arena[:, OFF_P3A:OFF_P3A + 8192].bitcast(BF16), [128, 16384], BF16)})
        if upto == "P3a":
            finish()
            return nc
        barrier()

        ar = Arena(0)
        x1 = ar.f32(16 * 1024).rearrange("p (i n) -> p i n", i=16)
        h2T = ar.bf(8 * 2048).rearrange("p (c n) -> p c n", c=8)
        xn2 = [ar.bf(1024) for _ in range(2)]
        junk2 = ar.bf(1024)
        assert ar.off <= OFF_P3A, (ar.off, OFF_P3A)
        ar = Arena(OFF_YEND)
        wout = ar.bf(8 * 1024).rearrange("p (c n) -> p c n", c=8)
        xr = [ar.f32(1024) for _ in range(2)]

        dma_pool(wout, w_out.rearrange("(c p) n -> p c n", p=128), "wout")
        for c in range(8):
            tt("dve", wout[:, c, :], wout[:, c, :], g1b[:], ALU.mult, ["wout", "g1b"], ["wout"])
        def T1(i):
            s2 = i % 2
            dma_sp(xr[s2], xp[i * 128:(i + 1) * 128, :], "xr%d" % s2)
            yk = ["yT_%d_%d" % (dc_, i // 4) for dc_ in range(8)]
            for half in range(2):
                pb_ = (2 * i + half) % 4
                for dc in range(8):
                    mm(PS(pb_), yT[:, dc, i * 128:(i + 1) * 128], wout[:, dc, half * 512:(half + 1) * 512], dc == 0, dc == 7,
                       yk + ["wout"], [pk(pb_)])
                tt("dve", x1[:, i, half * 512:(half + 1) * 512], PS(pb_), xr[s2][:, half * 512:(half + 1) * 512], ALU.add,
                   [pk(pb_), "xr%d" % s2], ["x1_%d_%d" % (i, half)])

        def T2(i):
            s2 = i % 2
            xk = ["x1_%d_0" % i, "x1_%d_1" % i]
            act(junk2, x1[:, i, :], AF.Square, xk + ["st_init"], ["ss2_%d" % i], accum_out=st[:, SS2 + i:SS2 + i + 1])
            rsqrt_col(SS2 + i, RS2 + i, 1.0 / D, ["ss2_%d" % i], "rs2_%d" % i)
            tsc("dve", xn2[s2], x1[:, i, :], st[:, RS2 + i:RS2 + i + 1], None, ALU.mult, None, xk + ["rs2_%d" % i], ["xn2_%d" % s2])

        def T3(i):
            s2 = i % 2
            tp = 4 + s2
            for c in range(8):
                tr(PSB(tp)[:, c * 128:(c + 1) * 128], xn2[s2][:, c * 128:(c + 1) * 128], ["xn2_%d" % s2], [pk(tp)])
            for c in range(8):
                dest = h2T[:, c, i * 128:(i + 1) * 128]
                src_ = PSB(tp)[:, c * 128:(c + 1) * 128]
                if c < 4:
                    act(dest, src_, AF.Identity, [pk(tp), "G2", "modT"], ["h2T_%d_%d" % (i, c)], scale=G2[:, c:c + 1], bias=modT3[:, 24 + c, 0:1])
                else:
                    tsc("dve", dest, src_, G2[:, c:c + 1], modT3[:, 24 + c, 0:1], ALU.mult, ALU.add, [pk(tp), "G2", "modT"],
                        ["h2T_%d_%d" % (i, c)])

        tst = [T1, T2, T3]
        for it_ in range(16 + len(tst) - 1):
            for si in reversed(range(len(tst))):
                i = it_ - si
                if 0 <= i < 16:
                    tst[si](i)

        DBG.update({"x1": (arena[:, 0:16384], [128, 16384], F32), "h2T": (arena[:, 16384:16384 + 8192].bitcast(BF16), [128, 16384], BF16)})
        if upto == "P3b":
            finish()
            return nc
        barrier()

        ar = Arena(OFF_P3A)
        wu = [ar.bf(8 * 512).rearrange("p (c n) -> p c n", c=8) for _ in range(2)]
        wd = [ar.bf(4 * 1024).rearrange("p (c n) -> p c n", c=4) for _ in range(2)]
        uT = [ar.bf(4 * 512).rearrange("p (c n) -> p c n", c=4) for _ in range(2)]
        rT = [ar.bf(512) for _ in range(2)]
        w_up3 = w_up.rearrange("(c p) n -> p c n", p=128)
        w_dn3 = w_down.rearrange("(c p) n -> p c n", p=128)

        def load_group(g):
            s_ = g % 2
            dma_pool(wu[s_], w_up3[:, :, g * 512:(g + 1) * 512], "wu%d" % s_)
            dma_pool(wd[s_], w_dn3[:, 4 * g:4 * g + 4, :], "wd%d" % s_)
            tt("dve", wd[s_], wd[s_], g2b[:].unsqueeze(1).broadcast_to([128, 4, 1024]), ALU.mult, ["wd%d" % s_, "g2b"], ["wd%d" % s_])

        load_group(0)
        load_group(1)
        cntb = [0]

        def up(u):
            g, blk = u // 4, u % 4
            s_ = g % 2
            us = u % 2
            tok = slice(blk * 512, (blk + 1) * 512)
            h2k = ["h2T_%d_%d" % (t_, c_) for t_ in range(4 * blk, 4 * blk + 4) for c_ in range(8)]
            for fcl in range(4):
                pu = cntb[0] % 2
                cntb[0] += 1
                for c in range(8):
                    mm(PS(pu), wu[s_][:, c, fcl * 128:(fcl + 1) * 128], h2T[:, c, tok], c == 0, c == 7, ["wu%d" % s_] + h2k, [pk(pu)])
                act(rT[pu], PS(pu), AF.Relu, [pk(pu)], ["rT%d" % pu])
                tt("pool", uT[us][:, fcl, :], rT[pu], rT[pu], ALU.mult, ["rT%d" % pu], ["uT%d_%d" % (us, fcl)])

        def down(u):
            g, blk = u // 4, u % 4
            s_ = g % 2
            us = u % 2
            uk = ["uT%d_%d" % (us, f_) for f_ in range(4)]
            for i in range(4):
                ti_ = 4 * blk + i
                for half in range(2):
                    pd = 2 + (2 * i + half) % 4
                    for fcl in range(4):
                        mm(PS(pd), uT[us][:, fcl, i * 128:(i + 1) * 128], wd[s_][:, fcl, half * 512:(half + 1) * 512], fcl == 0, fcl == 3,
                           uk + ["wd%d" % s_], [pk(pd)])
                    hs = slice(half * 512, (half + 1) * 512)
                    tt("dve", x1[:, ti_, hs], PS(pd), x1[:, ti_, hs], ALU.add, [pk(pd), "x1_%d_%d" % (ti_, half)], ["x1_%d_%d" % (ti_, half)])
            if blk == 3 and g + 2 < 8:
                load_group(g + 2)

        for u in range(33):
            if u < 32:
                up(u)
            if u >= 1:
                down(u - 1)
        for i in range(16):
            dma_sp(out_d[i * 128:(i + 1) * 128, :], x1[:, i, :], "out%d" % i, reads=["x1_%d_0" % i, "x1_%d_1" % i])
            fin.append("out%d" % i)
        finish()
    return nc


def _host_tables(h):
    bf = ml_dtypes.bfloat16
    a = np.arange(64)
    k1 = 32 * h + np.arange(32)
    ph = 2 * np.pi * np.outer(a, k1) / 64.0
    W1h = np.concatenate([np.cos(ph), -np.sin(ph)], axis=1)
    W1 = np.zeros((128, 128), np.float64)
    W1[0:64, 0:64] = W1h
    W1[64:128, 64:128] = W1h
    W1 = W1.astype(np.float32).astype(bf)
    c = np.arange(128)
    pc = 2 * np.pi * np.outer(c, c) / 128.0
    C, Sn = np.cos(pc), np.sin(pc)
    CS = np.concatenate([C, -Sn, Sn, C], axis=1).astype(np.float32).astype(bf)
    nrm = 1.0 / np.sqrt(4096.0 * 128.0)
    T2 = np.zeros((128, 16, 2, 2, 64), np.float64)
    k2 = np.arange(64)
    for kk in range(2):
        for n2p in range(64):
            s_ = n2p // 32
            bl = n2p % 32
            n2 = 32 * (h ^ s_) + bl
            for pair in range(16):
                k1_ = 32 * h + 2 * pair + kk
                th = 2 * np.pi * ((n2 * (k1_ + 64 * k2)) % 4096) / 4096.0
                T2[64 * kk + n2p, pair, kk, 0, :] = np.cos(th) * nrm
                T2[64 * kk + n2p, pair, kk, 1, :] = np.sin(th) * nrm
    T2 = T2.reshape(128, 4096).astype(np.float32).astype(bf)
    freqs = (np.float32(10000.0) ** (-np.arange(8, dtype=np.float32) / np.float32(8))).astype(np.float32)
    cosK = np.ones((128, 34, 32), np.float32)
    sinK = np.zeros((128, 34, 32), np.float32)
    p = np.arange(128)
    q = p // 64
    aa = p % 64
    sign = np.concatenate([-np.ones(8), np.ones(8), -np.ones(8), np.ones(8)]).astype(np.float32)
    for kt in range(32):
        s_, j = kt // 16, kt % 16
        bl = 2 * j + q
        n = 64 * aa + 32 * (h ^ s_) + bl
        row = (n // 64).astype(np.float32)
        col = (n % 64).astype(np.float32)
        ang_r = row[:, None] * freqs[None, :]
        ang_c = col[:, None] * freqs[None, :]
        ang = np.concatenate([ang_r, ang_r, ang_c, ang_c], axis=-1).astype(np.float32)
        cosK[:, kt, :] = np.cos(ang)
        sinK[:, kt, :] = np.sin(ang) * sign[None, :]
    return W1, CS, T2, cosK.reshape(128, -1), sinK.reshape(128, -1)


def _swap32(g):
    return np.concatenate([g[8:16], g[0:8], g[24:32], g[16:24]])


_NC_CACHE = {}


def kernel(x, c, ctx, c_ctx, ada_w, ada_b, norm1_g, norm2_g, w_in, b_gate, w_fourier, q_norm_g, w_qb,
           kv_norm_g, w_kvb, q_gain, k_gain, w_mla_o, w_out, w_up, w_down, _debug=(), _upto=None, _cores=None):
    f = lambda a: np.ascontiguousarray(np.asarray(a, dtype=np.float32))
    x, c, ctx, c_ctx = f(x), f(c), f(ctx), f(c_ctx)
    key = (tuple(_debug), _upto)
    if key not in _NC_CACHE:
        _NC_CACHE[key] = build(debug=_debug, upto=_upto)
    nc = _NC_CACHE[key]

    def cols(v, n):
        return np.ascontiguousarray(f(v).reshape(n, 128).T)

    kg, qg = f(k_gain)[0], f(q_gain)[0]
    gk96 = np.zeros((128, 1), np.float32); gk96[:64, 0] = kg[:64]; gk96[64:96, 0] = 1.0
    gq96 = np.zeros((128, 1), np.float32); gq96[:64, 0] = qg[:64]; gq96[64:96, 0] = 1.0
    gkr = np.ascontiguousarray(np.broadcast_to(np.concatenate([kg[64:96], _swap32(kg[64:96])])[None, :], (128, 64)))
    gqr = np.ascontiguousarray(np.broadcast_to(np.concatenate([qg[64:96], _swap32(qg[64:96])])[None, :], (128, 64)))
    sel = np.zeros((2, 130), np.float32); sel[0, 0] = 1; sel[1, 1] = 1; sel[0, 2:] = 1
    common = {
        "ada_w": f(ada_w)[0], "ada_b2": np.ascontiguousarray(np.stack([f(ada_b)[0], f(ada_b)[0]])),
        "n1g": cols(norm1_g[0], 8), "n2g": cols(norm2_g[0], 8), "bgate": cols(b_gate[0], 16),
        "qng": cols(q_norm_g[0], 2), "kvng": cols(kv_norm_g[0], 1), "gk96": gk96, "gq96": gq96, "gkr": gkr, "gqr": gqr,
        "w_in": f(w_in)[0], "w_fourier": f(w_fourier)[0], "w_qb": f(w_qb)[0], "w_kvb": f(w_kvb)[0], "w_mla_o": f(w_mla_o)[0],
        "w_out": f(w_out)[0], "w_up": f(w_up)[0], "w_down": f(w_down)[0],
        "ident": np.eye(128, dtype=np.float32).astype(ml_dtypes.bfloat16), "sel": sel,
    }
    tabs = [_host_tables(h) for h in range(2)]
    in_maps = []
    for core in range(8):
        b, h = core // 2, core % 2
        xb = x[b].reshape(64, 64, D)
        own = xb[:, 32 * h:32 * h + 32, :].transpose(1, 0, 2).reshape(NOWN, D)
        oth = xb[:, 32 * (1 - h):32 * (1 - h) + 32, :].transpose(1, 0, 2).reshape(NOWN, D)
        cv = np.stack([c[b], c_ctx], -1).reshape(8, 128, 2).transpose(1, 0, 2).reshape(128, 16)
        W1, CS, T2, cosK, sinK = tabs[h]
        m = dict(common)
        m.update({"xp": np.ascontiguousarray(np.concatenate([own, oth], 0)), "ctx": ctx[b], "cvec": np.ascontiguousarray(cv),
                  "W1": W1, "CS": CS, "T2": T2, "cosK": cosK, "sinK": sinK})
        in_maps.append(m)
    if _cores is not None:
        res = run_bass_kernel_spmd(nc, [in_maps[c_] for c_ in _cores], core_ids=list(range(len(_cores))))
        return None, res
    res = run_bass_kernel_spmd(nc, in_maps, core_ids=list(range(8)))
    out = np.empty((4, S, D), np.float32)
    for core in range(8):
        b, h = core // 2, core % 2
        oc = np.asarray(res.results[core]["out"]).reshape(32, 64, D).transpose(1, 0, 2)
        out[b].reshape(64, 64, D)[:, 32 * h:32 * h + 32, :] = oc
    if _debug or _upto:
        return out, res
    return out
```

```python
import os
import numpy as np
import ml_dtypes
from contextlib import ExitStack
import concourse.bass as bass
import concourse.mybir as mybir
from concourse.bass_utils import run_bass_kernel_spmd

F32 = mybir.dt.float32
BF16 = mybir.dt.bfloat16
F32R = mybir.dt.float32r
AF = mybir.ActivationFunctionType
ALU = mybir.AluOpType
AX = mybir.AxisListType

D = 1024
S = 4096
NOWN = 2048
DIN = 2976
EPS = 1e-6
SCALE = 96 ** -0.5


class _Op:
    __slots__ = ("eng", "fn", "deps", "sig", "is_dma", "semkey", "dma_cum", "needed")

    def __init__(self, eng, fn, is_dma=False, semkey=None):
        self.eng = eng
        self.fn = fn
        self.deps = []
        self.sig = None
        self.is_dma = is_dma
        self.semkey = semkey
        self.dma_cum = None
        self.needed = False


class Prog:
    ENGS = ("pe", "act", "dve", "pool", "sp")

    def __init__(self, nc):
        self.nc = nc
        self.ops = {e: [] for e in self.ENGS}
        self.last_w = {}
        self.readers = {}
        self.dma_counts = {}
        self.nbar = 0

    def add(self, eng, fn, reads=(), writes=(), dma=False, semkey=None):
        op = _Op(eng, fn, dma, semkey)
        deps = []
        for r in reads:
            w = self.last_w.get(r)
            if w is not None:
                deps.append(w)
            if isinstance(r, str) and r.startswith("ps") and eng != "pe":
                deps.extend(rd for rd in self.readers.get(r, ()) if rd.eng != eng)
        for w_ in writes:
            w = self.last_w.get(w_)
            if w is not None:
                deps.append(w)
            deps.extend(self.readers.get(w_, ()))
        seen = set()
        for d in deps:
            if id(d) in seen or d is op:
                continue
            seen.add(id(d))
            if (not d.is_dma) and d.eng == "pe" and eng == "pe" and not dma:
                continue
            op.deps.append(d)
            d.needed = True
        if dma:
            c = self.dma_counts.get(semkey, 0) + 16
            self.dma_counts[semkey] = c
            op.dma_cum = c
        for r in reads:
            self.readers.setdefault(r, []).append(op)
        for w_ in writes:
            self.last_w[w_] = op
            self.readers[w_] = []
        self.ops[eng].append(op)
        return op

    def barrier(self, fn):
        self.nbar += 1
        tag = "__bar%d" % self.nbar
        keys = [k for k in set(self.last_w) | set(self.readers) if not str(k).startswith("__bar")]
        self.add("dve", fn, writes=keys + [tag])
        for e in ("act", "pool", "sp", "pe"):
            self.add(e, None, reads=[tag])

    def emit(self, final_keys=()):
        nc = self.nc
        self.add("sp", None, reads=final_keys)
        with ExitStack() as es:
            esem = {e: es.enter_context(nc.semaphore("s_" + e)) for e in self.ENGS if e != "sp"}
            dsem = {k: es.enter_context(nc.semaphore("d_%d" % i)) for i, k in enumerate(self.dma_counts)}
            for e in self.ENGS:
                n = 0
                for op in self.ops[e]:
                    if op.is_dma:
                        continue
                    if op.needed:
                        n += 1
                        op.sig = n
            block = es.enter_context(nc.Block())

            def run(engh, e):
                waited = {}
                for op in self.ops[e]:
                    need = {}
                    for d in op.deps:
                        if d.is_dma:
                            s, v = dsem[d.semkey], d.dma_cum
                        else:
                            s, v = esem[d.eng], d.sig
                        k = id(s)
                        if k not in need or need[k][1] < v:
                            need[k] = (s, v)
                    for k, (s, v) in need.items():
                        if waited.get(k, 0) >= v:
                            continue
                        waited[k] = v
                        engh.wait_ge(s, v)
                    if op.fn is None:
                        continue
                    inst = op.fn(engh)
                    if op.is_dma:
                        inst.then_inc(dsem[op.semkey], 16)
                    elif op.needed:
                        inst.then_inc(esem[e], 1)

            @block.tensor
            def _(t):
                run(t, "pe")

            @block.scalar
            def _(t):
                run(t, "act")

            @block.vector
            def _(t):
                run(t, "dve")

            @block.gpsimd
            def _(t):
                run(t, "pool")

            @block.sync
            def _(t):
                run(t, "sp")


def build(debug=(), upto=None):
    nc = bass.Bass("TRN2", target_bir_lowering=False)

    def din(name, shape, dt=F32):
        return nc.dram_tensor(name, list(shape), dt, kind="ExternalInput").ap()

    xp = din("xp", [S, D])
    ctx = din("ctx", [256, D])
    cvec_d = din("cvec", [128, 16])
    ada_w = din("ada_w", [D, 6 * D])
    ada_b2 = din("ada_b2", [2, 6 * D])
    n1g_d = din("n1g", [128, 8])
    n2g_d = din("n2g", [128, 8])
    bgate_d = din("bgate", [128, 16])
    qng_d = din("qng", [128, 2])
    kvng_d = din("kvng", [128, 1])
    gk96_d = din("gk96", [128, 1])
    gq96_d = din("gq96", [128, 1])
    gkr_d = din("gkr", [128, 64])
    gqr_d = din("gqr", [128, 64])
    w_in = din("w_in", [D, DIN])
    w_fourier = din("w_fourier", [512, D])
    w_qb = din("w_qb", [256, 768])
    w_kvb = din("w_kvb", [128, 1024])
    w_mla_o = din("w_mla_o", [512, D])
    w_out = din("w_out", [D, D])
    w_up = din("w_up", [D, 4 * D])
    w_down = din("w_down", [4 * D, D])
    ident_d = din("ident", [128, 128], BF16)
    W1_d = din("W1", [128, 128], BF16)
    CS_d = din("CS", [128, 512], BF16)
    T2_d = din("T2", [128, 4096], BF16)
    cosK_d = din("cosK", [128, 34 * 32])
    sinK_d = din("sinK", [128, 34 * 32])
    sel_d = din("sel", [2, 130])
    out_d = nc.dram_tensor("out", [NOWN, D], F32, kind="ExternalOutput").ap()

    es = ExitStack()
    with es:
        def sb(name, shape, dt=F32):
            return es.enter_context(nc.sbuf_tensor("sb_" + name, list(shape), dt))

        ident = sb("ident", [128, 128], BF16)
        W1 = sb("W1", [128, 128], BF16)
        CS = sb("CS", [128, 512], BF16)
        cv = sb("cv", [128, 16])
        sg = sb("sg", [128, 16])
        scv = sb("scv", [128, 16], F32 if not os.environ.get("USEF32R") else F32R)
        modT = sb("modT", [128, 96])
        n1g = sb("n1g", [128, 8])
        n2g = sb("n2g", [128, 8])
        bgate = sb("bgate", [128, 16])
        qng = sb("qng", [128, 2])
        kvng = sb("kvng", [128, 1])
        gk96 = sb("gk96", [128, 1])
        gq96 = sb("gq96", [128, 1])
        gkr = sb("gkr", [128, 64])
        gqr = sb("gqr", [128, 64])
        G1 = sb("G1", [128, 8])
        G2 = sb("G2", [128, 8])
        cG1 = sb("cG1", [128, 8])
        g1b = sb("g1b", [128, 1024])
        g2b = sb("g2b", [128, 1024])
        sel = sb("sel", [2, 130])
        ones_f = sb("ones_f", [128, 128])
        mhalf = sb("mhalf", [128, 8])
        st = sb("st", [128, 256])
        cosK = sb("cosK", [128, 34 * 32])
        sinK = sb("sinK", [128, 34 * 32])
        cosQ = sb("cosQ", [128, 16 * 32])
        sinQ = sb("sinQ", [128, 16 * 32])
        ssn = sb("ssn", [128, 16])
        rk = sb("rk", [128, 16])
        ARW = 41472
        arena = sb("arena", [128, ARW])
        psall = es.enter_context(nc.psum_tensor("psall", [128, 4096], F32))

        def PS(i):
            return psall[:, i * 512:(i + 1) * 512]

        def PSB(i):
            return psall[:, i * 512:(i + 1) * 512].bitcast(BF16)

        def pk(i):
            return "ps%d" % i

        SSX, RSX, SSKV, RSKV, SSR, SSQ, RSQ, SS2, RS2 = 0, 34, 68, 102, 136, 170, 186, 202, 218

        class Arena:
            def __init__(self, off=0):
                self.off = off

            def f32(self, nwords):
                assert nwords % 2 == 0 and self.off % 2 == 0
                a = arena[:, self.off:self.off + nwords]
                self.off += nwords
                assert self.off <= ARW, self.off
                return a

            def bf(self, nelem):
                assert nelem % 2 == 0
                return self.f32(nelem // 2).bitcast(BF16)

        P = Prog(nc)
        A = P.add

        def dma_sp(out, in_, key, reads=()):
            A("sp", lambda e: e.dma_start(out=out, in_=in_), reads=reads, writes=[key], dma=True, semkey=key)

        def dma_pool(out, in_, key, reads=()):
            A("pool", lambda e: e.dma_start(out=out, in_=in_), reads=reads, writes=[key], dma=True, semkey=key)

        def act(out, in_, func, r, w, **kw):
            A("act", lambda e: e.activation(out=out, in_=in_, func=func, **kw), reads=r, writes=w)

        def tsc(eng, out, in0, s1, s2, op0, op1, r, w):
            if op1 is None:
                A(eng, lambda e: e.tensor_scalar(out=out, in0=in0, scalar1=s1, scalar2=None, op0=op0), reads=r, writes=w)
            else:
                A(eng, lambda e: e.tensor_scalar(out=out, in0=in0, scalar1=s1, scalar2=s2, op0=op0, op1=op1), reads=r, writes=w)

        def tt(eng, out, in0, in1, op, r, w):
            A(eng, lambda e: e.tensor_tensor(out=out, in0=in0, in1=in1, op=op), reads=r, writes=w)

        def cp(eng, out, in_, r, w):
            A(eng, lambda e: e.tensor_copy(out=out, in_=in_), reads=r, writes=w)

        def mm(out, lhsT, rhs, start, stop, r, w):
            A("pe", lambda e: e.matmul(out, lhsT=lhsT, rhs=rhs, start=start, stop=stop), reads=r, writes=w)

        def tr(out, in_, r, w):
            A("pe", lambda e: e.transpose(out=out, in_=in_, identity=ident[:]), reads=list(r) + ["ident"], writes=w)

        def rsqrt_col(col_ss, col_rs, mul, keys_r, key_w, n=1):
            tsc("dve", st[:, col_rs:col_rs + n], st[:, col_ss:col_ss + n], mul, EPS, ALU.mult, ALU.add, keys_r, [key_w])
            tt("pool", st[:, col_rs:col_rs + n], st[:, col_rs:col_rs + n], mhalf[:, 0:n], ALU.pow, [key_w, "mhalf"], [key_w])

        def barrier():
            if os.environ.get("NOBAR"):
                return
            P.barrier(lambda e: e.memset(ssn[:, 15:16], 0.0))

        DBG = {}
        fin = []

        def finish():
            for name in debug:
                ap_, shp, dt_ = DBG[name]
                dd = nc.dram_tensor("dbg_" + name, list(shp), dt_, kind="ExternalOutput").ap()
                dma_sp(dd, ap_, "dbg_" + name, reads=[k for k in P.last_w.keys() if not str(k).startswith("dbg_")])
                fin.append("dbg_" + name)
            P.emit(final_keys=fin)

        for (t_, d_, k_) in [(ident, ident_d, "ident"), (W1, W1_d, "W1"), (CS, CS_d, "CS"),
                             (cv, cvec_d, "cv"), (n1g, n1g_d, "n1g"), (n2g, n2g_d, "n2g"), (bgate, bgate_d, "bgate"),
                             (qng, qng_d, "qng"), (kvng, kvng_d, "kvng"), (gk96, gk96_d, "gk96"), (gq96, gq96_d, "gq96"),
                             (gkr, gkr_d, "gkr"), (gqr, gqr_d, "gqr"), (sel, sel_d, "sel"),
                             (cosK, cosK_d, "cosK"), (sinK, sinK_d, "sinK")]:
            dma_sp(t_[:], d_, k_)
        A("dve", lambda e: e.memset(st[:], 0.0), writes=["st_init"])
        A("dve", lambda e: e.memset(mhalf[:], -0.5), writes=["mhalf"])
        A("dve", lambda e: e.memset(ones_f[:], 1.0), writes=["ones_f"])
        A("dve", lambda e: e.memset(ssn[:], 0.0), writes=["ssn"])

        act(sg[:], cv[:], AF.Sigmoid, ["cv"], ["sg"])
        tt("dve", scv[:], cv[:], sg[:], ALU.mult, ["cv", "sg"], ["scv"])
        scv3 = scv[:].rearrange("p (c r) -> p c r", r=2)

        ar = Arena(0)
        modrow = ar.f32(6144)
        abias = ar.f32(6144)
        NSTG = 4
        stage = [ar.f32(4096).rearrange("p (c n) -> p c n", c=8) for _ in range(NSTG)]
        dma_sp(abias[0:2, :], ada_b2, "abias")
        adaw3 = ada_w.rearrange("(c p) n -> p c n", p=128)
        for blk in range(12):
            s_ = blk % NSTG
            dma_sp(stage[s_], adaw3[:, :, blk * 512:(blk + 1) * 512], "stage%d" % s_)
            pi = blk % 2
            for c in range(8):
                if not os.environ.get("USEF32R"):
                    mm(PS(pi)[0:2, :], scv3[:, c, :], stage[s_][:, c, :], c == 0, c == 7, ["scv", "stage%d" % s_], [pk(pi)])
                else:
                    mm(PS(pi)[0:2, :], scv3[:, c, :], stage[s_][:, c, :].bitcast(F32R), c == 0, c == 7,
                       ["scv", "stage%d" % s_], [pk(pi)])
            tt("dve", modrow[0:2, blk * 512:(blk + 1) * 512], PS(pi)[0:2, :], abias[0:2, blk * 512:(blk + 1) * 512], ALU.add,
               [pk(pi), "abias"], ["modrow%d" % blk])
        mrk = ["modrow%d" % b_ for b_ in range(12)]
        for ch in range(48):
            mm(PS(2)[:, 2 * ch:2 * ch + 2], modrow[0:2, ch * 128:(ch + 1) * 128], sel[0:2, 0:2], True, True,
               mrk + ["sel"], [pk(2)])
        cp("dve", modT[:], PS(2)[:, 0:96], [pk(2)], ["modT"])
        modT3 = modT[:].rearrange("p (c r) -> p c r", r=2)
        for half in range(2):
            mm(PS(3)[:, :], sel[0:2, 2:130], modrow[0:2, 2048 + half * 512:2048 + (half + 1) * 512], True, True, mrk + ["sel"], [pk(3)])
            cp("dve", g1b[:, half * 512:(half + 1) * 512], PS(3), [pk(3)], ["g1b"])
            mm(PS(4)[:, :], sel[0:2, 2:130], modrow[0:2, 5120 + half * 512:5120 + (half + 1) * 512], True, True, mrk + ["sel"], [pk(4)])
            cp("dve", g2b[:, half * 512:(half + 1) * 512], PS(4), [pk(4)], ["g2b"])

        def stt(out, in0, scalar, in1, op0, op1, r, w):
            A("dve", lambda e: e.scalar_tensor_tensor(out=out, in0=in0, scalar=scalar, in1=in1, op0=op0, op1=op1), reads=r, writes=w)

        stt(G1[:], modT3[:, 8:16, 0], 1.0, n1g[:], ALU.add, ALU.mult, ["modT", "n1g"], ["G1"])
        stt(cG1[:], modT3[:, 8:16, 1], 1.0, n1g[:], ALU.add, ALU.mult, ["modT", "n1g"], ["cG1"])
        stt(G2[:], modT3[:, 32:40, 0], 1.0, n2g[:], ALU.add, ALU.mult, ["modT", "n2g"], ["G2"])

        cosK3 = cosK[:].rearrange("p (t f) -> p t f", f=32)
        sinK3 = sinK[:].rearrange("p (t f) -> p t f", f=32)
        cosQ3 = cosQ[:].rearrange("p (t f) -> p t f", f=32)
        sinQ3 = sinQ[:].rearrange("p (t f) -> p t f", f=32)
        tt("dve", cosQ3, cosK3[:, 0:16, :], gqr[:, 0:32].unsqueeze(1).broadcast_to([128, 16, 32]), ALU.mult, ["cosK", "gqr"], ["cosQ"])
        tt("dve", sinQ3, sinK3[:, 0:16, :], gqr[:, 32:64].unsqueeze(1).broadcast_to([128, 16, 32]), ALU.mult, ["sinK", "gqr"], ["sinQ"])
        tt("dve", cosK3, cosK3, gkr[:, 0:32].unsqueeze(1).broadcast_to([128, 34, 32]), ALU.mult, ["cosK", "gkr", "cosQ"], ["cosK"])
        tt("dve", sinK3, sinK3, gkr[:, 32:64].unsqueeze(1).broadcast_to([128, 34, 32]), ALU.mult, ["sinK", "gkr", "sinQ"], ["sinK"])

        DBG.update({"modT": (modT[:], [128, 96], F32), "g1b": (g1b[:], [128, 1024], F32), "g2b": (g2b[:], [128, 1024], F32),
                    "G1": (G1[:], [128, 8], F32), "cosK": (cosK[:], [128, 34 * 32], F32), "sinQ": (sinQ[:], [128, 512], F32)})
        if upto == "P0":
            finish()
            return nc
        barrier()

        ar = Arena(0)
        kvnT = ar.bf(34 * 128)
        krr = ar.f32(34 * 32).rearrange("p (t f) -> p t f", f=32)
        qnT = ar.bf(2 * 2048).rearrange("p (c n) -> p c n", c=2)
        OFF_DEAD3 = ar.off
        fmixT = ar.bf(4 * 2048).rearrange("p (g n) -> p g n", g=4)
        hxTown = ar.bf(8 * 2048).rearrange("p (c n) -> p c n", c=8)
        wkvb = ar.bf(1024)
        wqb = ar.bf(2 * 768).rearrange("p (c n) -> p c n", c=2)
        OFF_P2 = ar.off
        winA = ar.bf(8 * 928).rearrange("p (c n) -> p c n", c=8)
        xt = [ar.f32(1024) for _ in range(2)]
        junk = ar.bf(1024)
        xn = [ar.bf(1024) for _ in range(2)]
        hxT = [ar.bf(1024).rearrange("p (c n) -> p c n", c=8) for _ in range(2)]
        Ftok = [ar.bf(512) for _ in range(2)]
        Hall = ar.bf(4 * 4096).rearrange("p (g r n) -> p g r n", g=4, r=64)
        Zg = [ar.bf(4096).rearrange("p (a m) -> p a m", a=16) for _ in range(1)]
        kvn = [ar.bf(128) for _ in range(2)]
        qn = [ar.bf(256) for _ in range(2)]
        ktmp = [ar.f32(32) for _ in range(4)]
        T2 = ar.bf(4096)

        dma_sp(T2, T2_d, "T2")
        w_in3 = w_in.rearrange("(c p) n -> p c n", p=128)
        if not os.environ.get("NOPOOLW"):
            dma_pool(winA, w_in3[:, :, 0:928], "winA")
            dma_pool(wkvb, w_kvb, "wkvb")
            dma_pool(wqb, w_qb.rearrange("(c p) n -> p c n", p=128), "wqb")

        def tinfo(t):
            return t < 16, t >= 32, t % 2

        def hkeys_of(t):
            own, isctx, s2 = tinfo(t)
            return [("hxTown_%d_%d" % (t, c)) if own else ("hxT%d_%d" % (s2, c)) for c in range(8)]

        def S1(t):
            own, isctx, s2 = tinfo(t)
            kxt = "xt%d" % s2
            src = ctx[(t - 32) * 128:(t - 31) * 128, :] if isctx else xp[t * 128:(t + 1) * 128, :]
            dma_sp(xt[s2], src, kxt)
            act(junk, xt[s2], AF.Square, [kxt, "st_init"], ["ssx%d" % t], accum_out=st[:, SSX + t:SSX + t + 1])
            rsqrt_col(SSX + t, RSX + t, 1.0 / D, ["ssx%d" % t], "rsx%d" % t)
            tsc("dve", xn[s2], xt[s2], st[:, RSX + t:RSX + t + 1], None, ALU.mult, None, [kxt, "rsx%d" % t], ["xn%d" % s2])

        def S2(t):
            own, isctx, s2 = tinfo(t)
            tp = s2
            for c in range(8):
                tr(PSB(tp)[:, c * 128:(c + 1) * 128], xn[s2][:, c * 128:(c + 1) * 128], ["xn%d" % s2], [pk(tp)])
            Gm = cG1 if isctx else G1
            rr = 1 if isctx else 0
            hk = hkeys_of(t)
            for c in range(8):
                dest = hxTown[:, c, t * 128:(t + 1) * 128] if own else hxT[s2][:, c, :]
                src_ = PSB(tp)[:, c * 128:(c + 1) * 128]
                if c < 4:
                    act(dest, src_, AF.Identity, [pk(tp), "G1", "cG1", "modT"], [hk[c]],
                        scale=Gm[:, c:c + 1], bias=modT3[:, c, rr:rr + 1])
                else:
                    tsc("dve", dest, src_, Gm[:, c:c + 1], modT3[:, c, rr:rr + 1], ALU.mult, ALU.add,
                        [pk(tp), "G1", "cG1", "modT"], [hk[c]])

        def S3(t):
            own, isctx, s2 = tinfo(t)
            hk = hkeys_of(t)

            def hx(c):
                return hxTown[:, c, t * 128:(t + 1) * 128] if own else hxT[s2][:, c, :]
            bA0 = 2 + 2 * s2
            bA1 = 3 + 2 * s2
            if not isctx:
                for c in range(8):
                    mm(PS(bA0), hx(c), winA[:, c, 0:512], c == 0, c == 7, hk + ["winA"], [pk(bA0)])
            c0 = 512 if own else 768
            for c in range(8):
                mm(PS(bA1)[:, c0 - 512:416], hx(c), winA[:, c, c0:928], c == 0, c == 7, hk + ["winA"], [pk(bA1)])
            if not isctx:
                kf = "Ftok%d" % s2
                act(Ftok[s2], PS(bA0), AF.Copy, [pk(bA0)], [kf])
                for g in range(4):
                    mm(PS(6)[:, g * 128:(g + 1) * 128], Ftok[s2][:, g * 128:(g + 1) * 128], W1[:, :], True, True, [kf, "W1"], [pk(6)])
            act(junk[:, 0:128], PS(bA1)[:, 256:384], AF.Square, [pk(bA1), "st_init"], ["sskv%d" % t],
                accum_out=st[:, SSKV + t:SSKV + t + 1])
            act(junk[:, 128:160], PS(bA1)[:, 384:416], AF.Square, [pk(bA1), "st_init"], ["ssr%d" % t],
                accum_out=st[:, SSR + t:SSR + t + 1])
            if own:
                act(junk[:, 256:512], PS(bA1)[:, 0:256], AF.Square, [pk(bA1), "st_init"], ["ssq%d" % t],
                    accum_out=st[:, SSQ + t:SSQ + t + 1])
            rsqrt_col(SSKV + t, RSKV + t, 1.0 / 128, ["sskv%d" % t], "rskv%d" % t)
            if own:
                rsqrt_col(SSQ + t, RSQ + t, 1.0 / 256, ["ssq%d" % t], "rsq%d" % t)

        def S4(t):
            own, isctx, s2 = tinfo(t)
            bA1 = 3 + 2 * s2
            if not isctx:
                sidx = t // 16
                j = t % 16
                for q in range(2):
                    n2 = 32 * sidx + 2 * j + q
                    cp("dve", Hall[:, :, :, n2], PS(6).rearrange("p (g q r) -> p g q r", g=4, q=2)[:, :, q, :],
                       [pk(6)], ["Hall_%d_%d" % (t, q)])
            tsc("dve", kvn[s2], PS(bA1)[:, 256:384], st[:, RSKV + t:RSKV + t + 1], None, ALU.mult, None, [pk(bA1), "rskv%d" % t],
                ["kvn%d" % s2])
            kr4 = PS(bA1)[:, 384:416].rearrange("p (a h f) -> p a h f", a=2, h=2)
            sn4 = sinK3[:, t, :].rearrange("p (a h f) -> p a h f", a=2, h=2)
            u1 = ktmp[s2]
            u2 = ktmp[2 + s2]
            u24 = u2.rearrange("p (a h f) -> p a h f", a=2, h=2)
            tt("dve", u1, PS(bA1)[:, 384:416], cosK3[:, t, :], ALU.mult, [pk(bA1), "cosK"], ["u1_%d" % s2])
            tt("dve", u24[:, :, 0, :], kr4[:, :, 1, :], sn4[:, :, 0, :], ALU.mult, [pk(bA1), "sinK"], ["u2a_%d" % s2])
            tt("dve", u24[:, :, 1, :], kr4[:, :, 0, :], sn4[:, :, 1, :], ALU.mult, [pk(bA1), "sinK"], ["u2b_%d" % s2])
            tt("pool", krr[:, t, :], u1, u2, ALU.add, ["u1_%d" % s2, "u2a_%d" % s2, "u2b_%d" % s2], ["krr%d" % t])
            if own:
                tsc("dve", qn[s2], PS(bA1)[:, 0:256], st[:, RSQ + t:RSQ + t + 1], None, ALU.mult, None, [pk(bA1), "rsq%d" % t],
                    ["qn%d" % s2])

        def S5(t):
            own, isctx, s2 = tinfo(t)
            tr(PSB(7)[:, 0:128], kvn[s2], ["kvn%d" % s2], [pk(7)])
            if own:
                for c in range(2):
                    tr(PSB(7)[:, 128 + c * 128:256 + c * 128], qn[s2][:, c * 128:(c + 1) * 128], ["qn%d" % s2], [pk(7)])
            act(kvnT[:, t * 128:(t + 1) * 128], PSB(7)[:, 0:128], AF.Identity, [pk(7), "kvng"], ["kvnT%d" % t], scale=kvng[:, 0:1])
            if own:
                for c in range(2):
                    act(qnT[:, c, t * 128:(t + 1) * 128], PSB(7)[:, 128 + c * 128:256 + c * 128], AF.Identity, [pk(7), "qng"],
                        ["qnT%d_%d" % (t, c)], scale=qng[:, c:c + 1])

        stages = [S1, S2, S3, S4, S5]
        NT = 34
        for it_ in range(NT + len(stages) - 1):
            for si in reversed(range(len(stages))):
                t = it_ - si
                if 0 <= t < NT:
                    stages[si](t)

        hallk = ["Hall_%d_%d" % (t, q) for t in range(32) for q in range(2)]
        T23 = T2.rearrange("p (a k r n) -> p a k r n", a=16, k=2, r=2)
        for g in range(4):
            zs = 0
            kz = "Zg%d" % zs
            for pair in range(16):
                pz = pair % 2
                mm(PS(pz)[:, 0:256], Hall[:, g, 2 * pair:2 * pair + 2, :].rearrange("p a n -> p (a n)"), CS[:, 0:256], True, False,
                   hallk + ["CS"], [pk(pz)])
                mm(PS(pz)[:, 0:256], Hall[:, g, 32 + 2 * pair:32 + 2 * pair + 2, :].rearrange("p a n -> p (a n)"), CS[:, 256:512], False, True,
                   hallk + ["CS"], [pk(pz)])
                if pair % 2 == 0:
                    act(Zg[zs][:, pair, :], PS(pz)[:, 0:256], AF.Copy, [pk(pz)], [kz + "_%d" % pair])
                else:
                    cp("dve", Zg[zs][:, pair, :], PS(pz)[:, 0:256], [pk(pz)], [kz + "_%d" % pair])
            for oct_ in range(4):
                py = 2 + oct_ % 2
                for i8 in range(8):
                    k1l = oct_ * 8 + i8
                    pair, kk = k1l // 2, k1l % 2
                    mm(PS(py)[:, i8 * 64:(i8 + 1) * 64], Zg[zs][:, pair, 0:128], T23[:, pair, kk, 0, :], True, False,
                       [kz + "_%d" % pair, "T2"], [pk(py)])
                    mm(PS(py)[:, i8 * 64:(i8 + 1) * 64], Zg[zs][:, pair, 128:256], T23[:, pair, kk, 1, :], False, True,
                       [kz + "_%d" % pair, "T2"], [pk(py)])
                if oct_ % 2 == 0:
                    act(fmixT[:, g, oct_ * 512:(oct_ + 1) * 512], PS(py), AF.Copy, [pk(py)], ["fmixT_%d_%d" % (g, oct_)])
                else:
                    cp("dve", fmixT[:, g, oct_ * 512:(oct_ + 1) * 512], PS(py), [pk(py)], ["fmixT_%d_%d" % (g, oct_)])

        DBG.update({"st": (st[:], [128, 256], F32), "kvnT": (kvnT, [128, 4352], BF16), "krr": (arena[:, 2176:2176 + 1088], [128, 1088], F32),
                    "qnT": (arena[:, 3264:3264 + 2048].bitcast(BF16), [128, 4096], BF16),
                    "fmixT": (arena[:, 5312:5312 + 4096].bitcast(BF16), [128, 8192], BF16),
                    "hxTown": (arena[:, 9408:9408 + 8192].bitcast(BF16), [128, 16384], BF16)})
        if upto == "P1":
            finish()
            return nc
        barrier()

        ar = Arena(OFF_P2)
        attT = ar.bf(8 * 2048)[:, 0:4 * 2048].rearrange("p (h n) -> p h n", h=4)
        OFF_P3A = ar.off
        KT = [ar.bf(4352) for _ in range(2)]
        QT = [ar.bf(2048) for _ in range(2)]
        Vh = [ar.bf(34 * 128).rearrange("p (t f) -> p t f", f=128) for _ in range(2)]
        pT = [ar.bf(1024) for _ in range(2)]
        sqs = ar.f32(384)
        khb = [ar.bf(4 * 96).rearrange("p (i f) -> p i f", f=96) for _ in range(2)]
        qtmp = [ar.f32(128) for _ in range(2)]
        Rb = ar.f32(512)
        rcp = ar.f32(512)

        for s_ in range(2):
            A("pool", lambda e, s_=s_: e.memset(KT[s_], 0.0), writes=["KT%d" % s_])
            A("pool", lambda e, s_=s_: e.memset(QT[s_], 0.0), writes=["QT%d" % s_])
            A("dve", lambda e, s_=s_: e.memset(Vh[s_], 0.0), writes=["Vones%d" % s_])
            oc = 64 if s_ == 0 else 0
            A("dve", lambda e, s_=s_, oc=oc: e.memset(Vh[s_][:, :, oc:oc + 1], 1.0), writes=["Vones%d" % s_])

        def gen_kv_mm(h, grp, slot):
            tiles = list(range(4 * grp, min(4 * grp + 4, 34)))
            for i, kt in enumerate(tiles):
                mm(PS(6)[:, i * 128:(i + 1) * 128], kvnT[:, kt * 128:(kt + 1) * 128], wkvb[:, h * 128:(h + 1) * 128], True, True,
                   ["kvnT%d" % kt, "wkvb"], [pk(6)])
            n = len(tiles)
            t0 = tiles[0]
            kv3 = PS(6)[:, 0:n * 128].rearrange("p (i f) -> p i f", f=128)
            sq3 = sqs[:, 0:n * 64].rearrange("p (i f) -> p i f", f=64)
            act(sq3, kv3[:, :, 0:64], AF.Square, [pk(6)], ["sqs"])
            vc = 0 if slot == 0 else 64
            cp("dve", Vh[slot][:, t0:t0 + n, vc:vc + 64], kv3[:, :, 64:128], [pk(6), "Vones%d" % slot], ["V%d" % slot])
            A("dve", lambda e: e.tensor_reduce(out=ssn[:, 0:n], in_=sq3, axis=AX.X, op=ALU.add), reads=["sqs"], writes=["ssn"])
            tt("dve", ssn[:, 0:n], ssn[:, 0:n], st[:, SSR + t0:SSR + t0 + n], ALU.add, ["ssn"] + ["ssr%d" % k for k in tiles], ["ssn"])
            tsc("dve", rk[:, 0:n], ssn[:, 0:n], 1.0 / 96, EPS, ALU.mult, ALU.add, ["ssn"], ["rk"])
            tt("pool", rk[:, 0:n], rk[:, 0:n], mhalf[:, 0:n], ALU.pow, ["rk", "mhalf"], ["rk"])
            kb = khb[grp % 2]
            kkb = "khb%d" % (grp % 2)
            tt("dve", kb[:, 0:n, 0:64], kv3[:, :, 0:64], rk[:, 0:n].unsqueeze(2).broadcast_to([128, n, 64]), ALU.mult, [pk(6), "rk"], [kkb])
            tt("dve", kb[:, 0:n, 64:96], krr[:, t0:t0 + n, :], rk[:, 0:n].unsqueeze(2).broadcast_to([128, n, 32]), ALU.mult,
               ["krr%d" % k for k in tiles] + ["rk"], [kkb])
            return tiles

        def gen_kv_tr(h, grp, slot, tiles):
            n = len(tiles)
            t0 = tiles[0]
            kb = khb[grp % 2]
            kkb = "khb%d" % (grp % 2)
            for i in range(n):
                tr(PSB(7)[0:96, i * 128:(i + 1) * 128], kb[:, i, :], [kkb], [pk(7)])
            tsc("dve", KT[slot][0:96, t0 * 128:(t0 + n) * 128], PSB(7)[0:96, 0:n * 128], gk96[0:96, 0:1], None, ALU.mult, None,
                [pk(7), "gk96"], ["KT%d" % slot])

        def gen_q_mm(h, grp9, slot):
            grp = grp9 - 9
            for i in range(4):
                j = 4 * grp + i
                for c in range(2):
                    mm(PS(6)[:, i * 96:(i + 1) * 96], qnT[:, c, j * 128:(j + 1) * 128], wqb[:, c, h * 96:(h + 1) * 96], c == 0, c == 1,
                       ["qnT%d_%d" % (j, c), "wqb"], [pk(6)])
            q3 = PS(6)[:, 0:384].rearrange("p (i f) -> p i f", f=96)
            sq3 = sqs[:, 0:384].rearrange("p (i f) -> p i f", f=96)
            act(sq3, q3, AF.Square, [pk(6)], ["sqs"])
            A("dve", lambda e: e.tensor_reduce(out=ssn[:, 0:4], in_=sq3, axis=AX.X, op=ALU.add), reads=["sqs"], writes=["ssn"])
            tsc("dve", rk[:, 0:4], ssn[:, 0:4], 1.0 / 96, EPS, ALU.mult, ALU.add, ["ssn"], ["rk"])
            tt("pool", rk[:, 0:4], rk[:, 0:4], mhalf[:, 0:4], ALU.pow, ["rk", "mhalf"], ["rk"])
            kb = khb[grp9 % 2]
            kkb = "khb%d" % (grp9 % 2)
            tt("dve", kb[:, :, 0:64], q3[:, :, 0:64], rk[:, 0:4].unsqueeze(2).broadcast_to([128, 4, 64]), ALU.mult, [pk(6), "rk"], [kkb])
            j0 = 4 * grp
            qr = q3[:, :, 64:96]
            qr5 = qr.rearrange("p i (a h f) -> p i a h f", a=2, h=2)
            u1 = qtmp[0].rearrange("p (i f) -> p i f", f=32)
            u2 = qtmp[1].rearrange("p (i f) -> p i f", f=32)
            u25 = qtmp[1].rearrange("p (i a h f) -> p i a h f", i=4, a=2, h=2)
            sn5 = sinQ3[:, j0:j0 + 4, :].rearrange("p i (a h f) -> p i a h f", a=2, h=2)
            tt("dve", u1, qr, cosQ3[:, j0:j0 + 4, :], ALU.mult, [pk(6), "cosQ"], ["qu1"])
            for hh in range(2):
                for a_ in range(2):
                    tt("dve", u25[:, :, a_, hh, :], qr5[:, :, a_, 1 - hh, :], sn5[:, :, a_, hh, :], ALU.mult, [pk(6), "sinQ"],
                       ["qu2_%d%d" % (hh, a_)])
            tt("pool", u1, u1, u2, ALU.add, ["qu1"] + ["qu2_%d%d" % (hh, a_) for hh in range(2) for a_ in range(2)], ["qu1"])
            tt("dve", kb[:, :, 64:96], u1, rk[:, 0:4].unsqueeze(2).broadcast_to([128, 4, 32]), ALU.mult, ["qu1", "rk"], [kkb])

        def gen_q_tr(h, grp9, slot):
            grp = grp9 - 9
            kb = khb[grp9 % 2]
            kkb = "khb%d" % (grp9 % 2)
            for i in range(4):
                tr(PSB(7)[0:96, i * 128:(i + 1) * 128], kb[:, i, :], [kkb], [pk(7)])
            tsc("dve", QT[slot][0:96, grp * 512:(grp + 1) * 512], PSB(7)[0:96, 0:512], gq96[0:96, 0:1], None, ALU.mult, None,
                [pk(7), "gq96"], ["QT%d" % slot])

        def gen_tasks(h, slot):
            mms, trs = [], []
            for grp in range(9):
                box = {}
                mms.append(lambda grp=grp, box=box: box.__setitem__("t", gen_kv_mm(h, grp, slot)))
                trs.append(lambda grp=grp, box=box: gen_kv_tr(h, grp, slot, box["t"]))
            for grp in range(4):
                mms.append(lambda grp=grp: gen_q_mm(h, grp + 9, slot))
                trs.append(lambda grp=grp: gen_q_tr(h, grp + 9, slot))
            tasks = [mms[0]]
            for i in range(1, len(mms)):
                tasks.append(mms[i])
                tasks.append(trs[i - 1])
            tasks.append(trs[-1])
            return tasks

        for tk in gen_tasks(0, 0):
            tk()

        NPR = 17
        pending = []

        def make_norm(h, qb, ob):
            par = h % 2
            lo = 64 * par
            rs = 64 if par == 0 else 0

            def fn():
                A("dve", lambda e: e.reciprocal(out=rcp[0:1, :], in_=PS(ob)[rs:rs + 1, :]), reads=[pk(ob)], writes=["rcp"])
                mm(PS(7)[:, :], ones_f[0:1, 0:128], rcp[0:1, :], True, True, ["ones_f", "rcp"], [pk(7)])
                cp("dve", Rb[lo:lo + 64, :], PS(7)[lo:lo + 64, :], [pk(7)], ["Rb"])
                tt("dve", attT[lo:lo + 64, h // 2, qb * 512:(qb + 1) * 512], PS(ob)[lo:lo + 64, :], Rb[lo:lo + 64, :], ALU.mult,
                   [pk(ob), "Rb"], ["attT_%d_%d" % (h, qb)])
            return fn

        for h in range(8):
            slot = h % 2
            nxt = gen_tasks(h + 1, 1 - slot) if h < 7 else []
            total_steps = 4 * (NPR + 2)
            stepno = 0
            ti = 0
            for qb in range(4):
                ob = 4 + qb % 2
                for step in range(NPR + 2):
                    if step < NPR:
                        sb0 = 2 * (step % 2)
                        for u in range(2):
                            kt = 2 * step + u
                            mm(PS(sb0 + u), KT[slot][:, kt * 128:(kt + 1) * 128], QT[slot][:, qb * 512:(qb + 1) * 512], True, True,
                               ["KT%d" % slot, "QT%d" % slot], [pk(sb0), pk(sb0 + 1)] if u == 1 else [pk(sb0)])
                    if 1 <= step <= NPR:
                        pr = step - 1
                        sb0 = 2 * (pr % 2)
                        act(pT[pr % 2], psall[:, sb0 * 512:(sb0 + 2) * 512], AF.Exp, [pk(sb0), pk(sb0 + 1)], ["pT%d" % (pr % 2)], scale=SCALE)
                    if step >= 2:
                        pr = step - 2
                        for u in range(2):
                            kt = 2 * pr + u
                            mm(PS(ob)[:, :], Vh[slot][:, kt, :], pT[pr % 2][:, u * 512:(u + 1) * 512], kt == 0, kt == 2 * NPR - 1,
                               ["V%d" % slot, "Vones%d" % slot, "pT%d" % (pr % 2)], [pk(ob)])
                    if step == 4 and pending:
                        pending.pop(0)()
                    stepno += 1
                    if nxt and ti < len(nxt) and stepno * len(nxt) >= (ti + 1) * total_steps * 0.9:
                        nxt[ti]()
                        ti += 1
                pending.append(make_norm(h, qb, ob))
            while ti < len(nxt):
                nxt[ti]()
                ti += 1
        while pending:
            pending.pop(0)()

        DBG.update({"attT": (arena[:, OFF_P2:OFF_P2 + 8192].bitcast(BF16), [128, 16384], BF16),
                    "KT1": (KT[1], [128, 4352], BF16), "QT1": (QT[1], [128, 2048], BF16)})
        if upto == "P2":
            finish()
            return nc
        barrier()

        ar = Arena(0)
        t12 = [ar.f32(1024) for _ in range(2)]
        sfa = [ar.bf(1024) for _ in range(2)]
        wf = [ar.bf(4 * 128).rearrange("p (g n) -> p g n", g=4) for _ in range(2)]
        wm = [ar.bf(8 * 128)[:, 0:512].rearrange("p (h n) -> p h n", h=4) for _ in range(2)]
        assert ar.off <= OFF_DEAD3, (ar.off, OFF_DEAD3)
        ar = Arena(OFF_P3A)
        yT = ar.bf(8 * 2048).rearrange("p (c n) -> p c n", c=8)
        OFF_YEND = ar.off
        wg = [ar.bf(8 * 2 * 128).rearrange("p (c f n) -> p c f n", c=8, f=2) for _ in range(2)]

        w_f3 = w_fourier.rearrange("(g p) n -> p g n", p=128)
        w_m3 = w_mla_o.rearrange("(h p) n -> p h n", p=128)

        def load_dc(dc):
            s_ = dc % 2
            for fa in range(2):
                c0_ = 928 + fa * 1024 + dc * 128
                dma_pool(wg[s_][:, :, fa, :], w_in3[:, :, c0_:c0_ + 128], "wg%d_%d" % (s_, fa))
            dma_pool(wf[s_], w_f3[:, :, dc * 128:(dc + 1) * 128], "wf%d" % s_)
            dma_pool(wm[s_], w_m3[:, :, dc * 128:(dc + 1) * 128], "wm%d" % s_)

        load_dc(0)
        load_dc(1)
        wout = arena[:, OFF_YEND + 2048:OFF_YEND + 2048 + 4096].bitcast(BF16).rearrange("p (c n) -> p c n", c=8)
        dma_pool(wout, w_out.rearrange("(c p) n -> p c n", p=128), "wout")
        for c in range(8):
            tt("dve", wout[:, c, :], wout[:, c, :], g1b[:], ALU.mult, ["wout", "g1b"], ["wout"])
        it = 0
        for dc in range(8):
            s_ = dc % 2
            for blk in range(4):
                bs = 4 * (it % 2)
                b2 = it % 2
                it += 1
                tok = slice(blk * 512, (blk + 1) * 512)
                hk = ["hxTown_%d_%d" % (t_, c_) for t_ in range(4 * blk, 4 * blk + 4) for c_ in range(8)]
                for fa in range(2):
                    for c in range(8):
                        mm(PS(bs + fa), wg[s_][:, c, fa, :], hxTown[:, c, tok], c == 0, c == 7,
                           ["wg%d_%d" % (s_, fa)] + hk, [pk(bs + fa)])
                fk = ["fmixT_%d_%d" % (g_, blk) for g_ in range(4)]
                for g in range(4):
                    mm(PS(bs + 2), wf[s_][:, g, :], fmixT[:, g, tok], g == 0, g == 3, ["wf%d" % s_] + fk, [pk(bs + 2)])
                ak = ["attT_%d_%d" % (h_, blk) for h_ in range(8)]
                for hp in range(4):
                    mm(PS(bs + 3), wm[s_][:, hp, :], attT[:, hp, tok], hp == 0, hp == 3, ["wm%d" % s_] + ak, [pk(bs + 3)])
                sf = sfa[b2][:, 0:512]
                sa = sfa[b2][:, 512:1024]
                t1 = t12[b2][:, 0:512]
                t2 = t12[b2][:, 512:1024]
                act(sf, PS(bs), AF.Sigmoid, [pk(bs), "bgate"], ["sf%d" % b2], bias=bgate[:, dc:dc + 1])
                act(sa, PS(bs + 1), AF.Sigmoid, [pk(bs + 1), "bgate"], ["sa%d" % b2], bias=bgate[:, 8 + dc:9 + dc])
                tt("dve", t1, PS(bs + 2), sf, ALU.mult, [pk(bs + 2), "sf%d" % b2], ["t1_%d" % b2])
                tt("dve", t2, PS(bs + 3), sa, ALU.mult, [pk(bs + 3), "sa%d" % b2], ["t2_%d" % b2])
                tt("pool", yT[:, dc, tok], t1, t2, ALU.add, ["t1_%d" % b2, "t2_%d" % b2], ["yT_%d_%d" % (dc, blk)])
            if dc + 2 < 8:
                load_dc(dc + 2)

        DBG.update({"yT": (arena[:, OFF_P3A:OFF_P3A + 8192].bitcast(BF16), [128, 16384], BF16)})
        if upto == "P3a":
            finish()
            return nc
        barrier()

        ar = Arena(0)
        x1 = ar.f32(16 * 1024).rearrange("p (i n) -> p i n", i=16)
        h2T = ar.bf(8 * 2048).rearrange("p (c n) -> p c n", c=8)
        xn2 = [ar.bf(1024) for _ in range(2)]
        junk2 = ar.bf(1024)
        assert ar.off <= OFF_P3A, (ar.off, OFF_P3A)
        ar = Arena(OFF_YEND)
        xr = [ar.f32(1024) for _ in range(2)]
        def T1(i):
            s2 = i % 2
            dma_sp(xr[s2], xp[i * 128:(i + 1) * 128, :], "xr%d" % s2)
            yk = ["yT_%d_%d" % (dc_, i // 4) for dc_ in range(8)]
            for half in range(2):
                pb_ = (2 * i + half) % 4
                for dc in range(8):
                    mm(PS(pb_), yT[:, dc, i * 128:(i + 1) * 128], wout[:, dc, half * 512:(half + 1) * 512], dc == 0, dc == 7,
                       yk + ["wout"], [pk(pb_)])
                tt("dve", x1[:, i, half * 512:(half + 1) * 512], PS(pb_), xr[s2][:, half * 512:(half + 1) * 512], ALU.add,
                   [pk(pb_), "xr%d" % s2], ["x1_%d_%d" % (i, half)])

        def T2(i):
            s2 = i % 2
            xk = ["x1_%d_0" % i, "x1_%d_1" % i]
            act(junk2, x1[:, i, :], AF.Square, xk + ["st_init"], ["ss2_%d" % i], accum_out=st[:, SS2 + i:SS2 + i + 1])
            rsqrt_col(SS2 + i, RS2 + i, 1.0 / D, ["ss2_%d" % i], "rs2_%d" % i)
            tsc("dve", xn2[s2], x1[:, i, :], st[:, RS2 + i:RS2 + i + 1], None, ALU.mult, None, xk + ["rs2_%d" % i], ["xn2_%d" % s2])

        def T3(i):
            s2 = i % 2
            tp = 4 + s2
            for c in range(8):
                tr(PSB(tp)[:, c * 128:(c + 1) * 128], xn2[s2][:, c * 128:(c + 1) * 128], ["xn2_%d" % s2], [pk(tp)])
            for c in range(8):
                dest = h2T[:, c, i * 128:(i + 1) * 128]
                src_ = PSB(tp)[:, c * 128:(c + 1) * 128]
                if c < 4:
                    act(dest, src_, AF.Identity, [pk(tp), "G2", "modT"], ["h2T_%d_%d" % (i, c)], scale=G2[:, c:c + 1], bias=modT3[:, 24 + c, 0:1])
                else:
                    tsc("dve", dest, src_, G2[:, c:c + 1], modT3[:, 24 + c, 0:1], ALU.mult, ALU.add, [pk(tp), "G2", "modT"],
                        ["h2T_%d_%d" % (i, c)])

        tst = [T1, T2, T3]
        for it_ in range(16 + len(tst) - 1):
            for si in reversed(range(len(tst))):
                i = it_ - si
                if 0 <= i < 16:
                    tst[si](i)

        DBG.update({"x1": (arena[:, 0:16384], [128, 16384], F32), "h2T": (arena[:, 16384:16384 + 8192].bitcast(BF16), [128, 16384], BF16)})
        if upto == "P3b":
            finish()
            return nc
        barrier()

        ar = Arena(OFF_P3A)
        wu = [ar.bf(8 * 512).rearrange("p (c n) -> p c n", c=8) for _ in range(2)]
        wd = [ar.bf(4 * 1024).rearrange("p (c n) -> p c n", c=4) for _ in range(2)]
        uT = [ar.bf(4 * 512).rearrange("p (c n) -> p c n", c=4) for _ in range(2)]
        rT = [ar.bf(512) for _ in range(2)]
        w_up3 = w_up.rearrange("(c p) n -> p c n", p=128)
        w_dn3 = w_down.rearrange("(c p) n -> p c n", p=128)

        def load_group(g):
            s_ = g % 2
            dma_pool(wu[s_], w_up3[:, :, g * 512:(g + 1) * 512], "wu%d" % s_)
            dma_pool(wd[s_], w_dn3[:, 4 * g:4 * g + 4, :], "wd%d" % s_)
            tt("dve", wd[s_], wd[s_], g2b[:].unsqueeze(1).broadcast_to([128, 4, 1024]), ALU.mult, ["wd%d" % s_, "g2b"], ["wd%d" % s_])

        load_group(0)
        load_group(1)
        cntb = [0]

        def up(u):
            g, blk = u // 4, u % 4
            s_ = g % 2
            us = u % 2
            tok = slice(blk * 512, (blk + 1) * 512)
            h2k = ["h2T_%d_%d" % (t_, c_) for t_ in range(4 * blk, 4 * blk + 4) for c_ in range(8)]
            for fcl in range(4):
                pu = cntb[0] % 2
                cntb[0] += 1
                for c in range(8):
                    mm(PS(pu), wu[s_][:, c, fcl * 128:(fcl + 1) * 128], h2T[:, c, tok], c == 0, c == 7, ["wu%d" % s_] + h2k, [pk(pu)])
                act(rT[pu], PS(pu), AF.Relu, [pk(pu)], ["rT%d" % pu])
                tt("pool", uT[us][:, fcl, :], rT[pu], rT[pu], ALU.mult, ["rT%d" % pu], ["uT%d_%d" % (us, fcl)])

        def down(u):
            g, blk = u // 4, u % 4
            s_ = g % 2
            us = u % 2
            uk = ["uT%d_%d" % (us, f_) for f_ in range(4)]
            for i in range(4):
                ti_ = 4 * blk + i
                for half in range(2):
                    pd = 2 + (2 * i + half) % 4
                    for fcl in range(4):
                        mm(PS(pd), uT[us][:, fcl, i * 128:(i + 1) * 128], wd[s_][:, fcl, half * 512:(half + 1) * 512], fcl == 0, fcl == 3,
                           uk + ["wd%d" % s_], [pk(pd)])
                    hs = slice(half * 512, (half + 1) * 512)
                    tt("dve", x1[:, ti_, hs], PS(pd), x1[:, ti_, hs], ALU.add, [pk(pd), "x1_%d_%d" % (ti_, half)], ["x1_%d_%d" % (ti_, half)])
            if blk == 3 and g + 2 < 8:
                load_group(g + 2)

        for u in range(33):
            if u < 32:
                up(u)
            if u >= 1:
                down(u - 1)
        for i in range(16):
            dma_sp(out_d[i * 128:(i + 1) * 128, :], x1[:, i, :], "out%d" % i, reads=["x1_%d_0" % i, "x1_%d_1" % i])
            fin.append("out%d" % i)
        finish()
    return nc


def _host_tables(h):
    bf = ml_dtypes.bfloat16
    a = np.arange(64)
    k1 = 32 * h + np.arange(32)
    ph = 2 * np.pi * np.outer(a, k1) / 64.0
    W1h = np.concatenate([np.cos(ph), -np.sin(ph)], axis=1)
    W1 = np.zeros((128, 128), np.float64)
    W1[0:64, 0:64] = W1h
    W1[64:128, 64:128] = W1h
    W1 = W1.astype(np.float32).astype(bf)
    c = np.arange(128)
    pc = 2 * np.pi * np.outer(c, c) / 128.0
    C, Sn = np.cos(pc), np.sin(pc)
    CS = np.concatenate([C, -Sn, Sn, C], axis=1).astype(np.float32).astype(bf)
    nrm = 1.0 / np.sqrt(4096.0 * 128.0)
    T2 = np.zeros((128, 16, 2, 2, 64), np.float64)
    k2 = np.arange(64)
    for kk in range(2):
        for n2p in range(64):
            s_ = n2p // 32
            bl = n2p % 32
            n2 = 32 * (h ^ s_) + bl
            for pair in range(16):
                k1_ = 32 * h + 2 * pair + kk
                th = 2 * np.pi * ((n2 * (k1_ + 64 * k2)) % 4096) / 4096.0
                T2[64 * kk + n2p, pair, kk, 0, :] = np.cos(th) * nrm
                T2[64 * kk + n2p, pair, kk, 1, :] = np.sin(th) * nrm
    T2 = T2.reshape(128, 4096).astype(np.float32).astype(bf)
    freqs = (np.float32(10000.0) ** (-np.arange(8, dtype=np.float32) / np.float32(8))).astype(np.float32)
    cosK = np.ones((128, 34, 32), np.float32)
    sinK = np.zeros((128, 34, 32), np.float32)
    p = np.arange(128)
    q = p // 64
    aa = p % 64
    sign = np.concatenate([-np.ones(8), np.ones(8), -np.ones(8), np.ones(8)]).astype(np.float32)
    for kt in range(32):
        s_, j = kt // 16, kt % 16
        bl = 2 * j + q
        n = 64 * aa + 32 * (h ^ s_) + bl
        row = (n // 64).astype(np.float32)
        col = (n % 64).astype(np.float32)
        ang_r = row[:, None] * freqs[None, :]
        ang_c = col[:, None] * freqs[None, :]
        ang = np.concatenate([ang_r, ang_r, ang_c, ang_c], axis=-1).astype(np.float32)
        cosK[:, kt, :] = np.cos(ang)
        sinK[:, kt, :] = np.sin(ang) * sign[None, :]
    return W1, CS, T2, cosK.reshape(128, -1), sinK.reshape(128, -1)


def _swap32(g):
    return np.concatenate([g[8:16], g[0:8], g[24:32], g[16:24]])


_NC_CACHE = {}


def kernel(x, c, ctx, c_ctx, ada_w, ada_b, norm1_g, norm2_g, w_in, b_gate, w_fourier, q_norm_g, w_qb,
           kv_norm_g, w_kvb, q_gain, k_gain, w_mla_o, w_out, w_up, w_down, _debug=(), _upto=None, _cores=None):
    f = lambda a: np.ascontiguousarray(np.asarray(a, dtype=np.float32))
    x, c, ctx, c_ctx = f(x), f(c), f(ctx), f(c_ctx)
    key = (tuple(_debug), _upto)
    if key not in _NC_CACHE:
        _NC_CACHE[key] = build(debug=_debug, upto=_upto)
    nc = _NC_CACHE[key]

    def cols(v, n):
        return np.ascontiguousarray(f(v).reshape(n, 128).T)

    kg, qg = f(k_gain)[0], f(q_gain)[0]
    gk96 = np.zeros((128, 1), np.float32); gk96[:64, 0] = kg[:64]; gk96[64:96, 0] = 1.0
    gq96 = np.zeros((128, 1), np.float32); gq96[:64, 0] = qg[:64]; gq96[64:96, 0] = 1.0
    gkr = np.ascontiguousarray(np.broadcast_to(np.concatenate([kg[64:96], _swap32(kg[64:96])])[None, :], (128, 64)))
    gqr = np.ascontiguousarray(np.broadcast_to(np.concatenate([qg[64:96], _swap32(qg[64:96])])[None, :], (128, 64)))
    sel = np.zeros((2, 130), np.float32); sel[0, 0] = 1; sel[1, 1] = 1; sel[0, 2:] = 1
    common = {
        "ada_w": f(ada_w)[0], "ada_b2": np.ascontiguousarray(np.stack([f(ada_b)[0], f(ada_b)[0]])),
        "n1g": cols(norm1_g[0], 8), "n2g": cols(norm2_g[0], 8), "bgate": cols(b_gate[0], 16),
        "qng": cols(q_norm_g[0], 2), "kvng": cols(kv_norm_g[0], 1), "gk96": gk96, "gq96": gq96, "gkr": gkr, "gqr": gqr,
        "w_in": f(w_in)[0], "w_fourier": f(w_fourier)[0], "w_qb": f(w_qb)[0], "w_kvb": f(w_kvb)[0], "w_mla_o": f(w_mla_o)[0],
        "w_out": f(w_out)[0], "w_up": f(w_up)[0], "w_down": f(w_down)[0],
        "ident": np.eye(128, dtype=np.float32).astype(ml_dtypes.bfloat16), "sel": sel,
    }
    tabs = [_host_tables(h) for h in range(2)]
    in_maps = []
    for core in range(8):
        b, h = core // 2, core % 2
        xb = x[b].reshape(64, 64, D)
        own = xb[:, 32 * h:32 * h + 32, :].transpose(1, 0, 2).reshape(NOWN, D)
        oth = xb[:, 32 * (1 - h):32 * (1 - h) + 32, :].transpose(1, 0, 2).reshape(NOWN, D)
        cv = np.stack([c[b], c_ctx], -1).reshape(8, 128, 2).transpose(1, 0, 2).reshape(128, 16)
        W1, CS, T2, cosK, sinK = tabs[h]
        m = dict(common)
        m.update({"xp": np.ascontiguousarray(np.concatenate([own, oth], 0)), "ctx": ctx[b], "cvec": np.ascontiguousarray(cv),
                  "W1": W1, "CS": CS, "T2": T2, "cosK": cosK, "sinK": sinK})
        in_maps.append(m)
    if _cores is not None:
        res = run_bass_kernel_spmd(nc, [in_maps[c_] for c_ in _cores], core_ids=list(range(len(_cores))))
        return None, res
    res = run_bass_kernel_spmd(nc, in_maps, core_ids=list(range(8)))
    out = np.empty((4, S, D), np.float32)
    for core in range(8):
        b, h = core // 2, core % 2
        oc = np.asarray(res.results[core]["out"]).reshape(32, 64, D).transpose(1, 0, 2)
        out[b].reshape(64, 64, D)[:, 32 * h:32 * h + 32, :] = oc
    if _debug or _upto:
        return out, res
    return out
```

```python
import os
import numpy as np
import ml_dtypes
from contextlib import ExitStack
import concourse.bass as bass
import concourse.mybir as mybir
from concourse.bass_utils import run_bass_kernel_spmd

F32 = mybir.dt.float32
BF16 = mybir.dt.bfloat16
F32R = mybir.dt.float32r
AF = mybir.ActivationFunctionType
ALU = mybir.AluOpType
AX = mybir.AxisListType

D = 1024
S = 4096
NOWN = 2048
DIN = 2976
EPS = 1e-6
SCALE = 96 ** -0.5


class _Op:
    __slots__ = ("eng", "fn", "deps", "sig", "is_dma", "semkey", "dma_cum", "needed")

    def __init__(self, eng, fn, is_dma=False, semkey=None):
        self.eng = eng
        self.fn = fn
        self.deps = []
        self.sig = None
        self.is_dma = is_dma
        self.semkey = semkey
        self.dma_cum = None
        self.needed = False


class Prog:
    ENGS = ("pe", "act", "dve", "pool", "sp")

    def __init__(self, nc):
        self.nc = nc
        self.ops = {e: [] for e in self.ENGS}
        self.last_w = {}
        self.readers = {}
        self.dma_counts = {}
        self.nbar = 0

    def add(self, eng, fn, reads=(), writes=(), dma=False, semkey=None):
        op = _Op(eng, fn, dma, semkey)
        deps = []
        for r in reads:
            w = self.last_w.get(r)
            if w is not None:
                deps.append(w)
            if isinstance(r, str) and r.startswith("ps") and eng != "pe":
                deps.extend(rd for rd in self.readers.get(r, ()) if rd.eng != eng)
        for w_ in writes:
            w = self.last_w.get(w_)
            if w is not None:
                deps.append(w)
            deps.extend(self.readers.get(w_, ()))
        seen = set()
        for d in deps:
            if id(d) in seen or d is op:
                continue
            seen.add(id(d))
            if (not d.is_dma) and d.eng == "pe" and eng == "pe" and not dma:
                continue
            op.deps.append(d)
            d.needed = True
        if dma:
            c = self.dma_counts.get(semkey, 0) + 16
            self.dma_counts[semkey] = c
            op.dma_cum = c
        for r in reads:
            self.readers.setdefault(r, []).append(op)
        for w_ in writes:
            self.last_w[w_] = op
            self.readers[w_] = []
        self.ops[eng].append(op)
        return op

    def barrier(self, fn):
        self.nbar += 1
        tag = "__bar%d" % self.nbar
        keys = [k for k in set(self.last_w) | set(self.readers) if not str(k).startswith("__bar")]
        self.add("dve", fn, writes=keys + [tag])
        for e in ("act", "pool", "sp", "pe"):
            self.add(e, None, reads=[tag])

    def emit(self, final_keys=()):
        nc = self.nc
        self.add("sp", None, reads=final_keys)
        with ExitStack() as es:
            esem = {e: es.enter_context(nc.semaphore("s_" + e)) for e in self.ENGS if e != "sp"}
            dsem = {k: es.enter_context(nc.semaphore("d_%d" % i)) for i, k in enumerate(self.dma_counts)}
            for e in self.ENGS:
                n = 0
                for op in self.ops[e]:
                    if op.is_dma:
                        continue
                    if op.needed:
                        n += 1
                        op.sig = n
            block = es.enter_context(nc.Block())

            def run(engh, e):
                waited = {}
                for op in self.ops[e]:
                    need = {}
                    for d in op.deps:
                        if d.is_dma:
                            s, v = dsem[d.semkey], d.dma_cum
                        else:
                            s, v = esem[d.eng], d.sig
                        k = id(s)
                        if k not in need or need[k][1] < v:
                            need[k] = (s, v)
                    for k, (s, v) in need.items():
                        if waited.get(k, 0) >= v:
                            continue
                        waited[k] = v
                        engh.wait_ge(s, v)
                    if op.fn is None:
                        continue
                    inst = op.fn(engh)
                    if op.is_dma:
                        inst.then_inc(dsem[op.semkey], 16)
                    elif op.needed:
                        inst.then_inc(esem[e], 1)

            @block.tensor
            def _(t):
                run(t, "pe")

            @block.scalar
            def _(t):
                run(t, "act")

            @block.vector
            def _(t):
                run(t, "dve")

            @block.gpsimd
            def _(t):
                run(t, "pool")

            @block.sync
            def _(t):
                run(t, "sp")


def build(debug=(), upto=None):
    nc = bass.Bass("TRN2", target_bir_lowering=False)

    def din(name, shape, dt=F32):
        return nc.dram_tensor(name, list(shape), dt, kind="ExternalInput").ap()

    xp = din("xp", [S, D])
    ctx = din("ctx", [256, D])
    cvec_d = din("cvec", [128, 16])
    ada_w = din("ada_w", [D, 6 * D])
    ada_b2 = din("ada_b2", [2, 6 * D])
    n1g_d = din("n1g", [128, 8])
    n2g_d = din("n2g", [128, 8])
    bgate_d = din("bgate", [128, 16])
    qng_d = din("qng", [128, 2])
    kvng_d = din("kvng", [128, 1])
    gk96_d = din("gk96", [128, 1])
    gq96_d = din("gq96", [128, 1])
    gkr_d = din("gkr", [128, 64])
    gqr_d = din("gqr", [128, 64])
    w_in = din("w_in", [D, DIN])
    w_fourier = din("w_fourier", [512, D])
    w_qb = din("w_qb", [256, 768])
    w_kvb = din("w_kvb", [128, 1024])
    w_mla_o = din("w_mla_o", [512, D])
    w_out = din("w_out", [D, D])
    w_up = din("w_up", [D, 4 * D])
    w_down = din("w_down", [4 * D, D])
    ident_d = din("ident", [128, 128], BF16)
    W1_d = din("W1", [128, 128], BF16)
    CS_d = din("CS", [128, 512], BF16)
    T2_d = din("T2", [128, 4096], BF16)
    cosK_d = din("cosK", [128, 34 * 32])
    sinK_d = din("sinK", [128, 34 * 32])
    sel_d = din("sel", [2, 130])
    out_d = nc.dram_tensor("out", [NOWN, D], F32, kind="ExternalOutput").ap()

    es = ExitStack()
    with es:
        def sb(name, shape, dt=F32):
            return es.enter_context(nc.sbuf_tensor("sb_" + name, list(shape), dt))

        ident = sb("ident", [128, 128], BF16)
        W1 = sb("W1", [128, 128], BF16)
        CS = sb("CS", [128, 512], BF16)
        cv = sb("cv", [128, 16])
        sg = sb("sg", [128, 16])
        scv = sb("scv", [128, 16], F32 if not os.environ.get("USEF32R") else F32R)
        modT = sb("modT", [128, 96])
        n1g = sb("n1g", [128, 8])
        n2g = sb("n2g", [128, 8])
        bgate = sb("bgate", [128, 16])
        qng = sb("qng", [128, 2])
        kvng = sb("kvng", [128, 1])
        gk96 = sb("gk96", [128, 1])
        gq96 = sb("gq96", [128, 1])
        gkr = sb("gkr", [128, 64])
        gqr = sb("gqr", [128, 64])
        G1 = sb("G1", [128, 8])
        G2 = sb("G2", [128, 8])
        cG1 = sb("cG1", [128, 8])
        g1b = sb("g1b", [128, 1024])
        g2b = sb("g2b", [128, 1024])
        sel = sb("sel", [2, 130])
        ones_f = sb("ones_f", [128, 128])
        mhalf = sb("mhalf", [128, 8])
        st = sb("st", [128, 256])
        cosK = sb("cosK", [128, 34 * 32])
        sinK = sb("sinK", [128, 34 * 32])
        cosQ = sb("cosQ", [128, 16 * 32])
        sinQ = sb("sinQ", [128, 16 * 32])
        ssn = sb("ssn", [128, 16])
        rk = sb("rk", [128, 16])
        ARW = 41472
        arena = sb("arena", [128, ARW])
        psall = es.enter_context(nc.psum_tensor("psall", [128, 4096], F32))

        def PS(i):
            return psall[:, i * 512:(i + 1) * 512]

        def PSB(i):
            return psall[:, i * 512:(i + 1) * 512].bitcast(BF16)

        def pk(i):
            return "ps%d" % i

        SSX, RSX, SSKV, RSKV, SSR, SSQ, RSQ, SS2, RS2 = 0, 34, 68, 102, 136, 170, 186, 202, 218

        class Arena:
            def __init__(self, off=0):
                self.off = off

            def f32(self, nwords):
                assert nwords % 2 == 0 and self.off % 2 == 0
                a = arena[:, self.off:self.off + nwords]
                self.off += nwords
                assert self.off <= ARW, self.off
                return a

            def bf(self, nelem):
                assert nelem % 2 == 0
                return self.f32(nelem // 2).bitcast(BF16)

        P = Prog(nc)
        A = P.add

        def dma_sp(out, in_, key, reads=()):
            A("sp", lambda e: e.dma_start(out=out, in_=in_), reads=reads, writes=[key], dma=True, semkey=key)

        def dma_pool(out, in_, key, reads=()):
            A("pool", lambda e: e.dma_start(out=out, in_=in_), reads=reads, writes=[key], dma=True, semkey=key)

        def act(out, in_, func, r, w, **kw):
            A("act", lambda e: e.activation(out=out, in_=in_, func=func, **kw), reads=r, writes=w)

        def tsc(eng, out, in0, s1, s2, op0, op1, r, w):
            if op1 is None:
                A(eng, lambda e: e.tensor_scalar(out=out, in0=in0, scalar1=s1, scalar2=None, op0=op0), reads=r, writes=w)
            else:
                A(eng, lambda e: e.tensor_scalar(out=out, in0=in0, scalar1=s1, scalar2=s2, op0=op0, op1=op1), reads=r, writes=w)

        def tt(eng, out, in0, in1, op, r, w):
            A(eng, lambda e: e.tensor_tensor(out=out, in0=in0, in1=in1, op=op), reads=r, writes=w)

        def cp(eng, out, in_, r, w):
            A(eng, lambda e: e.tensor_copy(out=out, in_=in_), reads=r, writes=w)

        def mm(out, lhsT, rhs, start, stop, r, w):
            A("pe", lambda e: e.matmul(out, lhsT=lhsT, rhs=rhs, start=start, stop=stop), reads=r, writes=w)

        def tr(out, in_, r, w):
            A("pe", lambda e: e.transpose(out=out, in_=in_, identity=ident[:]), reads=list(r) + ["ident"], writes=w)

        def rsqrt_col(col_ss, col_rs, mul, keys_r, key_w, n=1):
            tsc("dve", st[:, col_rs:col_rs + n], st[:, col_ss:col_ss + n], mul, EPS, ALU.mult, ALU.add, keys_r, [key_w])
            tt("pool", st[:, col_rs:col_rs + n], st[:, col_rs:col_rs + n], mhalf[:, 0:n], ALU.pow, [key_w, "mhalf"], [key_w])

        def barrier():
            if os.environ.get("NOBAR"):
                return
            P.barrier(lambda e: e.memset(ssn[:, 15:16], 0.0))

        DBG = {}
        fin = []

        def finish():
            for name in debug:
                ap_, shp, dt_ = DBG[name]
                dd = nc.dram_tensor("dbg_" + name, list(shp), dt_, kind="ExternalOutput").ap()
                dma_sp(dd, ap_, "dbg_" + name, reads=[k for k in P.last_w.keys() if not str(k).startswith("dbg_")])
                fin.append("dbg_" + name)
            P.emit(final_keys=fin)

        for (t_, d_, k_) in [(ident, ident_d, "ident"), (W1, W1_d, "W1"), (CS, CS_d, "CS"),
                             (cv, cvec_d, "cv"), (n1g, n1g_d, "n1g"), (n2g, n2g_d, "n2g"), (bgate, bgate_d, "bgate"),
                             (qng, qng_d, "qng"), (kvng, kvng_d, "kvng"), (gk96, gk96_d, "gk96"), (gq96, gq96_d, "gq96"),
                             (gkr, gkr_d, "gkr"), (gqr, gqr_d, "gqr"), (sel, sel_d, "sel"),
                             (cosK, cosK_d, "cosK"), (sinK, sinK_d, "sinK")]:
            dma_sp(t_[:], d_, k_)
        A("dve", lambda e: e.memset(st[:], 0.0), writes=["st_init"])
        A("dve", lambda e: e.memset(mhalf[:], -0.5), writes=["mhalf"])
        A("dve", lambda e: e.memset(ones_f[:], 1.0), writes=["ones_f"])
        A("dve", lambda e: e.memset(ssn[:], 0.0), writes=["ssn"])

        act(sg[:], cv[:], AF.Sigmoid, ["cv"], ["sg"])
        tt("dve", scv[:], cv[:], sg[:], ALU.mult, ["cv", "sg"], ["scv"])
        scv3 = scv[:].rearrange("p (c r) -> p c r", r=2)

        ar = Arena(0)
        modrow = ar.f32(6144)
        abias = ar.f32(6144)
        NSTG = 4
        stage = [ar.f32(4096).rearrange("p (c n) -> p c n", c=8) for _ in range(NSTG)]
        dma_sp(abias[0:2, :], ada_b2, "abias")
        adaw3 = ada_w.rearrange("(c p) n -> p c n", p=128)
        for blk in range(12):
            s_ = blk % NSTG
            dma_sp(stage[s_], adaw3[:, :, blk * 512:(blk + 1) * 512], "stage%d" % s_)
            pi = blk % 2
            for c in range(8):
                if not os.environ.get("USEF32R"):
                    mm(PS(pi)[0:2, :], scv3[:, c, :], stage[s_][:, c, :], c == 0, c == 7, ["scv", "stage%d" % s_], [pk(pi)])
                else:
                    mm(PS(pi)[0:2, :], scv3[:, c, :], stage[s_][:, c, :].bitcast(F32R), c == 0, c == 7,
                       ["scv", "stage%d" % s_], [pk(pi)])
            tt("dve", modrow[0:2, blk * 512:(blk + 1) * 512], PS(pi)[0:2, :], abias[0:2, blk * 512:(blk + 1) * 512], ALU.add,
               [pk(pi), "abias"], ["modrow%d" % blk])
        mrk = ["modrow%d" % b_ for b_ in range(12)]
        for ch in range(48):
            mm(PS(2)[:, 2 * ch:2 * ch + 2], modrow[0:2, ch * 128:(ch + 1) * 128], sel[0:2, 0:2], True, True,
               mrk + ["sel"], [pk(2)])
        cp("dve", modT[:], PS(2)[:, 0:96], [pk(2)], ["modT"])
        modT3 = modT[:].rearrange("p (c r) -> p c r", r=2)
        for half in range(2):
            mm(PS(3)[:, :], sel[0:2, 2:130], modrow[0:2, 2048 + half * 512:2048 + (half + 1) * 512], True, True, mrk + ["sel"], [pk(3)])
            cp("dve", g1b[:, half * 512:(half + 1) * 512], PS(3), [pk(3)], ["g1b"])
            mm(PS(4)[:, :], sel[0:2, 2:130], modrow[0:2, 5120 + half * 512:5120 + (half + 1) * 512], True, True, mrk + ["sel"], [pk(4)])
            cp("dve", g2b[:, half * 512:(half + 1) * 512], PS(4), [pk(4)], ["g2b"])

        def stt(out, in0, scalar, in1, op0, op1, r, w):
            A("dve", lambda e: e.scalar_tensor_tensor(out=out, in0=in0, scalar=scalar, in1=in1, op0=op0, op1=op1), reads=r, writes=w)

        stt(G1[:], modT3[:, 8:16, 0], 1.0, n1g[:], ALU.add, ALU.mult, ["modT", "n1g"], ["G1"])
        stt(cG1[:], modT3[:, 8:16, 1], 1.0, n1g[:], ALU.add, ALU.mult, ["modT", "n1g"], ["cG1"])
        stt(G2[:], modT3[:, 32:40, 0], 1.0, n2g[:], ALU.add, ALU.mult, ["modT", "n2g"], ["G2"])

        cosK3 = cosK[:].rearrange("p (t f) -> p t f", f=32)
        sinK3 = sinK[:].rearrange("p (t f) -> p t f", f=32)
        cosQ3 = cosQ[:].rearrange("p (t f) -> p t f", f=32)
        sinQ3 = sinQ[:].rearrange("p (t f) -> p t f", f=32)
        tt("dve", cosQ3, cosK3[:, 0:16, :], gqr[:, 0:32].unsqueeze(1).broadcast_to([128, 16, 32]), ALU.mult, ["cosK", "gqr"], ["cosQ"])
        tt("dve", sinQ3, sinK3[:, 0:16, :], gqr[:, 32:64].unsqueeze(1).broadcast_to([128, 16, 32]), ALU.mult, ["sinK", "gqr"], ["sinQ"])
        tt("dve", cosK3, cosK3, gkr[:, 0:32].unsqueeze(1).broadcast_to([128, 34, 32]), ALU.mult, ["cosK", "gkr", "cosQ"], ["cosK"])
        tt("dve", sinK3, sinK3, gkr[:, 32:64].unsqueeze(1).broadcast_to([128, 34, 32]), ALU.mult, ["sinK", "gkr", "sinQ"], ["sinK"])

        DBG.update({"modT": (modT[:], [128, 96], F32), "g1b": (g1b[:], [128, 1024], F32), "g2b": (g2b[:], [128, 1024], F32),
                    "G1": (G1[:], [128, 8], F32), "cosK": (cosK[:], [128, 34 * 32], F32), "sinQ": (sinQ[:], [128, 512], F32)})
        if upto == "P0":
            finish()
            return nc
        barrier()

        ar = Arena(0)
        kvnT = ar.bf(34 * 128)
        krr = ar.f32(34 * 32).rearrange("p (t f) -> p t f", f=32)
        qnT = ar.bf(2 * 2048).rearrange("p (c n) -> p c n", c=2)
        OFF_DEAD3 = ar.off
        fmixT = ar.bf(4 * 2048).rearrange("p (g n) -> p g n", g=4)
        hxTown = ar.bf(8 * 2048).rearrange("p (c n) -> p c n", c=8)
        wkvb = ar.bf(1024)
        wqb = ar.bf(2 * 768).rearrange("p (c n) -> p c n", c=2)
        OFF_P2 = ar.off
        winA = ar.bf(8 * 928).rearrange("p (c n) -> p c n", c=8)
        xt = [ar.f32(1024) for _ in range(2)]
        junk = ar.bf(1024)
        xn = [ar.bf(1024) for _ in range(2)]
        hxT = [ar.bf(1024).rearrange("p (c n) -> p c n", c=8) for _ in range(2)]
        Ftok = [ar.bf(512) for _ in range(2)]
        Hall = ar.bf(4 * 4096).rearrange("p (g r n) -> p g r n", g=4, r=64)
        Zg = [ar.bf(4096).rearrange("p (a m) -> p a m", a=16) for _ in range(1)]
        kvn = [ar.bf(128) for _ in range(2)]
        qn = [ar.bf(256) for _ in range(2)]
        ktmp = [ar.f32(32) for _ in range(4)]
        T2 = ar.bf(4096)

        dma_sp(T2, T2_d, "T2")
        w_in3 = w_in.rearrange("(c p) n -> p c n", p=128)
        if not os.environ.get("NOPOOLW"):
            dma_pool(winA, w_in3[:, :, 0:928], "winA")
            dma_pool(wkvb, w_kvb, "wkvb")
            dma_pool(wqb, w_qb.rearrange("(c p) n -> p c n", p=128), "wqb")

        def tinfo(t):
            return t < 16, t >= 32, t % 2

        def hkeys_of(t):
            own, isctx, s2 = tinfo(t)
            return [("hxTown_%d_%d" % (t, c)) if own else ("hxT%d_%d" % (s2, c)) for c in range(8)]

        def S1(t):
            own, isctx, s2 = tinfo(t)
            kxt = "xt%d" % s2
            src = ctx[(t - 32) * 128:(t - 31) * 128, :] if isctx else xp[t * 128:(t + 1) * 128, :]
            dma_sp(xt[s2], src, kxt)
            act(junk, xt[s2], AF.Square, [kxt, "st_init"], ["ssx%d" % t], accum_out=st[:, SSX + t:SSX + t + 1])
            rsqrt_col(SSX + t, RSX + t, 1.0 / D, ["ssx%d" % t], "rsx%d" % t)
            tsc("dve", xn[s2], xt[s2], st[:, RSX + t:RSX + t + 1], None, ALU.mult, None, [kxt, "rsx%d" % t], ["xn%d" % s2])

        def S2(t):
            own, isctx, s2 = tinfo(t)
            tp = s2
            for c in range(8):
                tr(PSB(tp)[:, c * 128:(c + 1) * 128], xn[s2][:, c * 128:(c + 1) * 128], ["xn%d" % s2], [pk(tp)])
            Gm = cG1 if isctx else G1
            rr = 1 if isctx else 0
            hk = hkeys_of(t)
            for c in range(8):
                dest = hxTown[:, c, t * 128:(t + 1) * 128] if own else hxT[s2][:, c, :]
                src_ = PSB(tp)[:, c * 128:(c + 1) * 128]
                if c < 4:
                    act(dest, src_, AF.Identity, [pk(tp), "G1", "cG1", "modT"], [hk[c]],
                        scale=Gm[:, c:c + 1], bias=modT3[:, c, rr:rr + 1])
                else:
                    tsc("dve", dest, src_, Gm[:, c:c + 1], modT3[:, c, rr:rr + 1], ALU.mult, ALU.add,
                        [pk(tp), "G1", "cG1", "modT"], [hk[c]])

        def S3(t):
            own, isctx, s2 = tinfo(t)
            hk = hkeys_of(t)

            def hx(c):
                return hxTown[:, c, t * 128:(t + 1) * 128] if own else hxT[s2][:, c, :]
            bA0 = 2 + 2 * s2
            bA1 = 3 + 2 * s2
            if not isctx:
                for c in range(8):
                    mm(PS(bA0), hx(c), winA[:, c, 0:512], c == 0, c == 7, hk + ["winA"], [pk(bA0)])
            c0 = 512 if own else 768
            for c in range(8):
                mm(PS(bA1)[:, c0 - 512:416], hx(c), winA[:, c, c0:928], c == 0, c == 7, hk + ["winA"], [pk(bA1)])
            if not isctx:
                kf = "Ftok%d" % s2
                act(Ftok[s2], PS(bA0), AF.Copy, [pk(bA0)], [kf])
                for g in range(4):
                    mm(PS(6)[:, g * 128:(g + 1) * 128], Ftok[s2][:, g * 128:(g + 1) * 128], W1[:, :], True, True, [kf, "W1"], [pk(6)])
            act(junk[:, 0:128], PS(bA1)[:, 256:384], AF.Square, [pk(bA1), "st_init"], ["sskv%d" % t],
                accum_out=st[:, SSKV + t:SSKV + t + 1])
            act(junk[:, 128:160], PS(bA1)[:, 384:416], AF.Square, [pk(bA1), "st_init"], ["ssr%d" % t],
                accum_out=st[:, SSR + t:SSR + t + 1])
            if own:
                act(junk[:, 256:512], PS(bA1)[:, 0:256], AF.Square, [pk(bA1), "st_init"], ["ssq%d" % t],
                    accum_out=st[:, SSQ + t:SSQ + t + 1])
            rsqrt_col(SSKV + t, RSKV + t, 1.0 / 128, ["sskv%d" % t], "rskv%d" % t)
            if own:
                rsqrt_col(SSQ + t, RSQ + t, 1.0 / 256, ["ssq%d" % t], "rsq%d" % t)

        def S4(t):
            own, isctx, s2 = tinfo(t)
            bA1 = 3 + 2 * s2
            if not isctx:
                sidx = t // 16
                j = t % 16
                for q in range(2):
                    n2 = 32 * sidx + 2 * j + q
                    cp("dve", Hall[:, :, :, n2], PS(6).rearrange("p (g q r) -> p g q r", g=4, q=2)[:, :, q, :],
                       [pk(6)], ["Hall_%d_%d" % (t, q)])
            tsc("dve", kvn[s2], PS(bA1)[:, 256:384], st[:, RSKV + t:RSKV + t + 1], None, ALU.mult, None, [pk(bA1), "rskv%d" % t],
                ["kvn%d" % s2])
            kr4 = PS(bA1)[:, 384:416].rearrange("p (a h f) -> p a h f", a=2, h=2)
            sn4 = sinK3[:, t, :].rearrange("p (a h f) -> p a h f", a=2, h=2)
            u1 = ktmp[s2]
            u2 = ktmp[2 + s2]
            u24 = u2.rearrange("p (a h f) -> p a h f", a=2, h=2)
            tt("dve", u1, PS(bA1)[:, 384:416], cosK3[:, t, :], ALU.mult, [pk(bA1), "cosK"], ["u1_%d" % s2])
            tt("dve", u24[:, :, 0, :], kr4[:, :, 1, :], sn4[:, :, 0, :], ALU.mult, [pk(bA1), "sinK"], ["u2a_%d" % s2])
            tt("dve", u24[:, :, 1, :], kr4[:, :, 0, :], sn4[:, :, 1, :], ALU.mult, [pk(bA1), "sinK"], ["u2b_%d" % s2])
            tt("pool", krr[:, t, :], u1, u2, ALU.add, ["u1_%d" % s2, "u2a_%d" % s2, "u2b_%d" % s2], ["krr%d" % t])
            if own:
                tsc("dve", qn[s2], PS(bA1)[:, 0:256], st[:, RSQ + t:RSQ + t + 1], None, ALU.mult, None, [pk(bA1), "rsq%d" % t],
                    ["qn%d" % s2])

        def S5(t):
            own, isctx, s2 = tinfo(t)
            tr(PSB(7)[:, 0:128], kvn[s2], ["kvn%d" % s2], [pk(7)])
            if own:
                for c in range(2):
                    tr(PSB(7)[:, 128 + c * 128:256 + c * 128], qn[s2][:, c * 128:(c + 1) * 128], ["qn%d" % s2], [pk(7)])
            act(kvnT[:, t * 128:(t + 1) * 128], PSB(7)[:, 0:128], AF.Identity, [pk(7), "kvng"], ["kvnT%d" % t], scale=kvng[:, 0:1])
            if own:
                for c in range(2):
                    act(qnT[:, c, t * 128:(t + 1) * 128], PSB(7)[:, 128 + c * 128:256 + c * 128], AF.Identity, [pk(7), "qng"],
                        ["qnT%d_%d" % (t, c)], scale=qng[:, c:c + 1])

        stages = [S1, S2, S3, S4, S5]
        NT = 34
        for it_ in range(NT + len(stages) - 1):
            for si in reversed(range(len(stages))):
                t = it_ - si
                if 0 <= t < NT:
                    stages[si](t)

        hallk = ["Hall_%d_%d" % (t, q) for t in range(32) for q in range(2)]
        T23 = T2.rearrange("p (a k r n) -> p a k r n", a=16, k=2, r=2)
        for g in range(4):
            zs = 0
            kz = "Zg%d" % zs
            for pair in range(16):
                pz = pair % 2
                mm(PS(pz)[:, 0:256], Hall[:, g, 2 * pair:2 * pair + 2, :].rearrange("p a n -> p (a n)"), CS[:, 0:256], True, False,
                   hallk + ["CS"], [pk(pz)])
                mm(PS(pz)[:, 0:256], Hall[:, g, 32 + 2 * pair:32 + 2 * pair + 2, :].rearrange("p a n -> p (a n)"), CS[:, 256:512], False, True,
                   hallk + ["CS"], [pk(pz)])
                if pair % 2 == 0:
                    act(Zg[zs][:, pair, :], PS(pz)[:, 0:256], AF.Copy, [pk(pz)], [kz + "_%d" % pair])
                else:
                    cp("dve", Zg[zs][:, pair, :], PS(pz)[:, 0:256], [pk(pz)], [kz + "_%d" % pair])
            for oct_ in range(4):
                py = 2 + oct_ % 2
                for i8 in range(8):
                    k1l = oct_ * 8 + i8
                    pair, kk = k1l // 2, k1l % 2
                    mm(PS(py)[:, i8 * 64:(i8 + 1) * 64], Zg[zs][:, pair, 0:128], T23[:, pair, kk, 0, :], True, False,
                       [kz + "_%d" % pair, "T2"], [pk(py)])
                    mm(PS(py)[:, i8 * 64:(i8 + 1) * 64], Zg[zs][:, pair, 128:256], T23[:, pair, kk, 1, :], False, True,
                       [kz + "_%d" % pair, "T2"], [pk(py)])
                if oct_ % 2 == 0:
                    act(fmixT[:, g, oct_ * 512:(oct_ + 1) * 512], PS(py), AF.Copy, [pk(py)], ["fmixT_%d_%d" % (g, oct_)])
                else:
                    cp("dve", fmixT[:, g, oct_ * 512:(oct_ + 1) * 512], PS(py), [pk(py)], ["fmixT_%d_%d" % (g, oct_)])

        DBG.update({"st": (st[:], [128, 256], F32), "kvnT": (kvnT, [128, 4352], BF16), "krr": (arena[:, 2176:2176 + 1088], [128, 1088], F32),
                    "qnT": (arena[:, 3264:3264 + 2048].bitcast(BF16), [128, 4096], BF16),
                    "fmixT": (arena[:, 5312:5312 + 4096].bitcast(BF16), [128, 8192], BF16),
                    "hxTown": (arena[:, 9408:9408 + 8192].bitcast(BF16), [128, 16384], BF16)})
        if upto == "P1":
            finish()
            return nc
        barrier()

        ar = Arena(OFF_P2)
        attT = ar.bf(8 * 2048)[:, 0:4 * 2048].rearrange("p (h n) -> p h n", h=4)
        OFF_P3A = ar.off
        KT = [ar.bf(4352) for _ in range(2)]
        QT = [ar.bf(2048) for _ in range(2)]
        Vh = [ar.bf(34 * 128).rearrange("p (t f) -> p t f", f=128) for _ in range(2)]
        pT = [ar.bf(1024) for _ in range(2)]
        sqs = ar.f32(384)
        khb = [ar.bf(4 * 96).rearrange("p (i f) -> p i f", f=96) for _ in range(2)]
        qtmp = [ar.f32(128) for _ in range(2)]
        Rb = ar.f32(512)
        rcp = ar.f32(512)

        for s_ in range(2):
            A("pool", lambda e, s_=s_: e.memset(KT[s_], 0.0), writes=["KT%d" % s_])
            A("pool", lambda e, s_=s_: e.memset(QT[s_], 0.0), writes=["QT%d" % s_])
            A("dve", lambda e, s_=s_: e.memset(Vh[s_], 0.0), writes=["Vones%d" % s_])
            oc = 64 if s_ == 0 else 0
            A("dve", lambda e, s_=s_, oc=oc: e.memset(Vh[s_][:, :, oc:oc + 1], 1.0), writes=["Vones%d" % s_])

        def gen_kv_mm(h, grp, slot):
            tiles = list(range(4 * grp, min(4 * grp + 4, 34)))
            for i, kt in enumerate(tiles):
                mm(PS(6)[:, i * 128:(i + 1) * 128], kvnT[:, kt * 128:(kt + 1) * 128], wkvb[:, h * 128:(h + 1) * 128], True, True,
                   ["kvnT%d" % kt, "wkvb"], [pk(6)])
            n = len(tiles)
            t0 = tiles[0]
            kv3 = PS(6)[:, 0:n * 128].rearrange("p (i f) -> p i f", f=128)
            sq3 = sqs[:, 0:n * 64].rearrange("p (i f) -> p i f", f=64)
            act(sq3, kv3[:, :, 0:64], AF.Square, [pk(6)], ["sqs"])
            vc = 0 if slot == 0 else 64
            cp("dve", Vh[slot][:, t0:t0 + n, vc:vc + 64], kv3[:, :, 64:128], [pk(6), "Vones%d" % slot], ["V%d" % slot])
            A("dve", lambda e: e.tensor_reduce(out=ssn[:, 0:n], in_=sq3, axis=AX.X, op=ALU.add), reads=["sqs"], writes=["ssn"])
            tt("dve", ssn[:, 0:n], ssn[:, 0:n], st[:, SSR + t0:SSR + t0 + n], ALU.add, ["ssn"] + ["ssr%d" % k for k in tiles], ["ssn"])
            tsc("dve", rk[:, 0:n], ssn[:, 0:n], 1.0 / 96, EPS, ALU.mult, ALU.add, ["ssn"], ["rk"])
            tt("pool", rk[:, 0:n], rk[:, 0:n], mhalf[:, 0:n], ALU.pow, ["rk", "mhalf"], ["rk"])
            kb = khb[grp % 2]
            kkb = "khb%d" % (grp % 2)
            tt("dve", kb[:, 0:n, 0:64], kv3[:, :, 0:64], rk[:, 0:n].unsqueeze(2).broadcast_to([128, n, 64]), ALU.mult, [pk(6), "rk"], [kkb])
            tt("dve", kb[:, 0:n, 64:96], krr[:, t0:t0 + n, :], rk[:, 0:n].unsqueeze(2).broadcast_to([128, n, 32]), ALU.mult,
               ["krr%d" % k for k in tiles] + ["rk"], [kkb])
            return tiles

        def gen_kv_tr(h, grp, slot, tiles):
            n = len(tiles)
            t0 = tiles[0]
            kb = khb[grp % 2]
            kkb = "khb%d" % (grp % 2)
            for i in range(n):
                tr(PSB(7)[0:96, i * 128:(i + 1) * 128], kb[:, i, :], [kkb], [pk(7)])
            tsc("dve", KT[slot][0:96, t0 * 128:(t0 + n) * 128], PSB(7)[0:96, 0:n * 128], gk96[0:96, 0:1], None, ALU.mult, None,
                [pk(7), "gk96"], ["KT%d" % slot])

        def gen_q_mm(h, grp9, slot):
            grp = grp9 - 9
            for i in range(4):
                j = 4 * grp + i
                for c in range(2):
                    mm(PS(6)[:, i * 96:(i + 1) * 96], qnT[:, c, j * 128:(j + 1) * 128], wqb[:, c, h * 96:(h + 1) * 96], c == 0, c == 1,
                       ["qnT%d_%d" % (j, c), "wqb"], [pk(6)])
            q3 = PS(6)[:, 0:384].rearrange("p (i f) -> p i f", f=96)
            sq3 = sqs[:, 0:384].rearrange("p (i f) -> p i f", f=96)
            act(sq3, q3, AF.Square, [pk(6)], ["sqs"])
            A("dve", lambda e: e.tensor_reduce(out=ssn[:, 0:4], in_=sq3, axis=AX.X, op=ALU.add), reads=["sqs"], writes=["ssn"])
            tsc("dve", rk[:, 0:4], ssn[:, 0:4], 1.0 / 96, EPS, ALU.mult, ALU.add, ["ssn"], ["rk"])
            tt("pool", rk[:, 0:4], rk[:, 0:4], mhalf[:, 0:4], ALU.pow, ["rk", "mhalf"], ["rk"])
            kb = khb[grp9 % 2]
            kkb = "khb%d" % (grp9 % 2)
            tt("dve", kb[:, :, 0:64], q3[:, :, 0:64], rk[:, 0:4].unsqueeze(2).broadcast_to([128, 4, 64]), ALU.mult, [pk(6), "rk"], [kkb])
            j0 = 4 * grp
            qr = q3[:, :, 64:96]
            qr5 = qr.rearrange("p i (a h f) -> p i a h f", a=2, h=2)
            u1 = qtmp[0].rearrange("p (i f) -> p i f", f=32)
            u2 = qtmp[1].rearrange("p (i f) -> p i f", f=32)
            u25 = qtmp[1].rearrange("p (i a h f) -> p i a h f", i=4, a=2, h=2)
            sn5 = sinQ3[:, j0:j0 + 4, :].rearrange("p i (a h f) -> p i a h f", a=2, h=2)
            tt("dve", u1, qr, cosQ3[:, j0:j0 + 4, :], ALU.mult, [pk(6), "cosQ"], ["qu1"])
            for hh in range(2):
                for a_ in range(2):
                    tt("dve", u25[:, :, a_, hh, :], qr5[:, :, a_, 1 - hh, :], sn5[:, :, a_, hh, :], ALU.mult, [pk(6), "sinQ"],
                       ["qu2_%d%d" % (hh, a_)])
            tt("pool", u1, u1, u2, ALU.add, ["qu1"] + ["qu2_%d%d" % (hh, a_) for hh in range(2) for a_ in range(2)], ["qu1"])
            tt("dve", kb[:, :, 64:96], u1, rk[:, 0:4].unsqueeze(2).broadcast_to([128, 4, 32]), ALU.mult, ["qu1", "rk"], [kkb])

        def gen_q_tr(h, grp9, slot):
            grp = grp9 - 9
            kb = khb[grp9 % 2]
            kkb = "khb%d" % (grp9 % 2)
            for i in range(4):
                tr(PSB(7)[0:96, i * 128:(i + 1) * 128], kb[:, i, :], [kkb], [pk(7)])
            tsc("dve", QT[slot][0:96, grp * 512:(grp + 1) * 512], PSB(7)[0:96, 0:512], gq96[0:96, 0:1], None, ALU.mult, None,
                [pk(7), "gq96"], ["QT%d" % slot])

        def gen_tasks(h, slot):
            mms, trs = [], []
            for grp in range(9):
                box = {}
                mms.append(lambda grp=grp, box=box: box.__setitem__("t", gen_kv_mm(h, grp, slot)))
                trs.append(lambda grp=grp, box=box: gen_kv_tr(h, grp, slot, box["t"]))
            for grp in range(4):
                mms.append(lambda grp=grp: gen_q_mm(h, grp + 9, slot))
                trs.append(lambda grp=grp: gen_q_tr(h, grp + 9, slot))
            tasks = [mms[0]]
            for i in range(1, len(mms)):
                tasks.append(mms[i])
                tasks.append(trs[i - 1])
            tasks.append(trs[-1])
            return tasks

        for tk in gen_tasks(0, 0):
            tk()

        arw = Arena(OFF_P2 + 4096)
        wg = [arw.bf(8 * 2 * 128).rearrange("p (c f n) -> p c f n", c=8, f=2) for _ in range(2)]
        wf = [arw.bf(4 * 128).rearrange("p (g n) -> p g n", g=4) for _ in range(2)]
        wm = [arw.bf(8 * 128)[:, 0:512].rearrange("p (h n) -> p h n", h=4) for _ in range(2)]
        assert arw.off <= OFF_P3A, (arw.off, OFF_P3A)
        w_f3 = w_fourier.rearrange("(g p) n -> p g n", p=128)
        w_m3 = w_mla_o.rearrange("(h p) n -> p h n", p=128)

        def load_dc(dc):
            s_ = dc % 2
            for fa in range(2):
                c0_ = 928 + fa * 1024 + dc * 128
                dma_pool(wg[s_][:, :, fa, :], w_in3[:, :, c0_:c0_ + 128], "wg%d_%d" % (s_, fa))
            dma_pool(wf[s_], w_f3[:, :, dc * 128:(dc + 1) * 128], "wf%d" % s_)
            dma_pool(wm[s_], w_m3[:, :, dc * 128:(dc + 1) * 128], "wm%d" % s_)

        NPR = 17
        pending = []

        def make_norm(h, qb, ob):
            par = h % 2
            lo = 64 * par
            rs = 64 if par == 0 else 0

            def fn():
                A("dve", lambda e: e.reciprocal(out=rcp[0:1, :], in_=PS(ob)[rs:rs + 1, :]), reads=[pk(ob)], writes=["rcp"])
                mm(PS(7)[:, :], ones_f[0:1, 0:128], rcp[0:1, :], True, True, ["ones_f", "rcp"], [pk(7)])
                cp("dve", Rb[lo:lo + 64, :], PS(7)[lo:lo + 64, :], [pk(7)], ["Rb"])
                tt("dve", attT[lo:lo + 64, h // 2, qb * 512:(qb + 1) * 512], PS(ob)[lo:lo + 64, :], Rb[lo:lo + 64, :], ALU.mult,
                   [pk(ob), "Rb"], ["attT_%d_%d" % (h, qb)])
            return fn

        for h in range(8):
            slot = h % 2
            nxt = gen_tasks(h + 1, 1 - slot) if h < 7 else []
            if h == 7:
                load_dc(0)
                load_dc(1)
            total_steps = 4 * (NPR + 2)
            stepno = 0
            ti = 0
            for qb in range(4):
                ob = 4 + qb % 2
                for step in range(NPR + 2):
                    if step < NPR:
                        sb0 = 2 * (step % 2)
                        for u in range(2):
                            kt = 2 * step + u
                            mm(PS(sb0 + u), KT[slot][:, kt * 128:(kt + 1) * 128], QT[slot][:, qb * 512:(qb + 1) * 512], True, True,
                               ["KT%d" % slot, "QT%d" % slot], [pk(sb0), pk(sb0 + 1)] if u == 1 else [pk(sb0)])
                    if 1 <= step <= NPR:
                        pr = step - 1
                        sb0 = 2 * (pr % 2)
                        act(pT[pr % 2], psall[:, sb0 * 512:(sb0 + 2) * 512], AF.Exp, [pk(sb0), pk(sb0 + 1)], ["pT%d" % (pr % 2)], scale=SCALE)
                    if step >= 2:
                        pr = step - 2
                        for u in range(2):
                            kt = 2 * pr + u
                            mm(PS(ob)[:, :], Vh[slot][:, kt, :], pT[pr % 2][:, u * 512:(u + 1) * 512], kt == 0, kt == 2 * NPR - 1,
                               ["V%d" % slot, "Vones%d" % slot, "pT%d" % (pr % 2)], [pk(ob)])
                    if step == 4 and pending:
                        pending.pop(0)()
                    stepno += 1
                    if nxt and ti < len(nxt) and stepno * len(nxt) >= (ti + 1) * total_steps * 0.9:
                        nxt[ti]()
                        ti += 1
                pending.append(make_norm(h, qb, ob))
            while ti < len(nxt):
                nxt[ti]()
                ti += 1
        while pending:
            pending.pop(0)()

        DBG.update({"attT": (arena[:, OFF_P2:OFF_P2 + 8192].bitcast(BF16), [128, 16384], BF16),
                    "KT1": (KT[1], [128, 4352], BF16), "QT1": (QT[1], [128, 2048], BF16)})
        if upto == "P2":
            finish()
            return nc
        barrier()

        ar = Arena(0)
        t12 = [ar.f32(1024) for _ in range(2)]
        sfa = [ar.bf(1024) for _ in range(2)]
        assert ar.off <= OFF_DEAD3, (ar.off, OFF_DEAD3)
        ar = Arena(OFF_P3A)
        yT = ar.bf(8 * 2048).rearrange("p (c n) -> p c n", c=8)
        OFF_YEND = ar.off
        wout = arena[:, OFF_YEND + 2048:OFF_YEND + 2048 + 4096].bitcast(BF16).rearrange("p (c n) -> p c n", c=8)
        dma_pool(wout, w_out.rearrange("(c p) n -> p c n", p=128), "wout")
        for c in range(8):
            tt("dve", wout[:, c, :], wout[:, c, :], g1b[:], ALU.mult, ["wout", "g1b"], ["wout"])
        it = 0
        for dc in range(8):
            s_ = dc % 2
            for blk in range(4):
                bs = 4 * (it % 2)
                b2 = it % 2
                it += 1
                tok = slice(blk * 512, (blk + 1) * 512)
                hk = ["hxTown_%d_%d" % (t_, c_) for t_ in range(4 * blk, 4 * blk + 4) for c_ in range(8)]
                for fa in range(2):
                    for c in range(8):
                        mm(PS(bs + fa), wg[s_][:, c, fa, :], hxTown[:, c, tok], c == 0, c == 7,
                           ["wg%d_%d" % (s_, fa)] + hk, [pk(bs + fa)])
                fk = ["fmixT_%d_%d" % (g_, blk) for g_ in range(4)]
                for g in range(4):
                    mm(PS(bs + 2), wf[s_][:, g, :], fmixT[:, g, tok], g == 0, g == 3, ["wf%d" % s_] + fk, [pk(bs + 2)])
                ak = ["attT_%d_%d" % (h_, blk) for h_ in range(8)]
                for hp in range(4):
                    mm(PS(bs + 3), wm[s_][:, hp, :], attT[:, hp, tok], hp == 0, hp == 3, ["wm%d" % s_] + ak, [pk(bs + 3)])
                sf = sfa[b2][:, 0:512]
                sa = sfa[b2][:, 512:1024]
                t1 = t12[b2][:, 0:512]
                t2 = t12[b2][:, 512:1024]
                act(sf, PS(bs), AF.Sigmoid, [pk(bs), "bgate"], ["sf%d" % b2], bias=bgate[:, dc:dc + 1])
                act(sa, PS(bs + 1), AF.Sigmoid, [pk(bs + 1), "bgate"], ["sa%d" % b2], bias=bgate[:, 8 + dc:9 + dc])
                tt("dve", t1, PS(bs + 2), sf, ALU.mult, [pk(bs + 2), "sf%d" % b2], ["t1_%d" % b2])
                tt("dve", t2, PS(bs + 3), sa, ALU.mult, [pk(bs + 3), "sa%d" % b2], ["t2_%d" % b2])
                tt("pool", yT[:, dc, tok], t1, t2, ALU.add, ["t1_%d" % b2, "t2_%d" % b2], ["yT_%d_%d" % (dc, blk)])
            if dc + 2 < 8:
                load_dc(dc + 2)

        DBG.update({"yT": (arena[:, OFF_P3A:OFF_P3A + 8192].bitcast(BF16), [128, 16384], BF16)})
        if upto == "P3a":
            finish()
            return nc
        barrier()

        ar = Arena(0)
        x1 = ar.f32(16 * 1024).rearrange("p (i n) -> p i n", i=16)
        h2T = ar.bf(8 * 2048).rearrange("p (c n) -> p c n", c=8)
        xn2 = [ar.bf(1024) for _ in range(2)]
        junk2 = ar.bf(1024)
        assert ar.off <= OFF_P3A, (ar.off, OFF_P3A)
        ar = Arena(OFF_YEND)
        xr = [ar.f32(1024) for _ in range(2)]
        def T1(i):
            s2 = i % 2
            dma_sp(xr[s2], xp[i * 128:(i + 1) * 128, :], "xr%d" % s2)
            yk = ["yT_%d_%d" % (dc_, i // 4) for dc_ in range(8)]
            for half in range(2):
                pb_ = (2 * i + half) % 4
                for dc in range(8):
                    mm(PS(pb_), yT[:, dc, i * 128:(i + 1) * 128], wout[:, dc, half * 512:(half + 1) * 512], dc == 0, dc == 7,
                       yk + ["wout"], [pk(pb_)])
                tt("dve", x1[:, i, half * 512:(half + 1) * 512], PS(pb_), xr[s2][:, half * 512:(half + 1) * 512], ALU.add,
                   [pk(pb_), "xr%d" % s2], ["x1_%d_%d" % (i, half)])

        def T2(i):
            s2 = i % 2
            xk = ["x1_%d_0" % i, "x1_%d_1" % i]
            act(junk2, x1[:, i, :], AF.Square, xk + ["st_init"], ["ss2_%d" % i], accum_out=st[:, SS2 + i:SS2 + i + 1])
            rsqrt_col(SS2 + i, RS2 + i, 1.0 / D, ["ss2_%d" % i], "rs2_%d" % i)
            tsc("dve", xn2[s2], x1[:, i, :], st[:, RS2 + i:RS2 + i + 1], None, ALU.mult, None, xk + ["rs2_%d" % i], ["xn2_%d" % s2])

        def T3(i):
            s2 = i % 2
            tp = 4 + s2
            for c in range(8):
                tr(PSB(tp)[:, c * 128:(c + 1) * 128], xn2[s2][:, c * 128:(c + 1) * 128], ["xn2_%d" % s2], [pk(tp)])
            for c in range(8):
                dest = h2T[:, c, i * 128:(i + 1) * 128]
                src_ = PSB(tp)[:, c * 128:(c + 1) * 128]
                if c < 4:
                    act(dest, src_, AF.Identity, [pk(tp), "G2", "modT"], ["h2T_%d_%d" % (i, c)], scale=G2[:, c:c + 1], bias=modT3[:, 24 + c, 0:1])
                else:
                    tsc("dve", dest, src_, G2[:, c:c + 1], modT3[:, 24 + c, 0:1], ALU.mult, ALU.add, [pk(tp), "G2", "modT"],
                        ["h2T_%d_%d" % (i, c)])

        tst = [T1, T2, T3]
        for it_ in range(16 + len(tst) - 1):
            for si in reversed(range(len(tst))):
                i = it_ - si
                if 0 <= i < 16:
                    tst[si](i)

        DBG.update({"x1": (arena[:, 0:16384], [128, 16384], F32), "h2T": (arena[:, 16384:16384 + 8192].bitcast(BF16), [128, 16384], BF16)})
        if upto == "P3b":
            finish()
            return nc
        barrier()

        ar = Arena(OFF_P3A)
        wu = [ar.bf(8 * 512).rearrange("p (c n) -> p c n", c=8) for _ in range(2)]
        wd = [ar.bf(4 * 1024).rearrange("p (c n) -> p c n", c=4) for _ in range(2)]
        uT = [ar.bf(4 * 512).rearrange("p (c n) -> p c n", c=4) for _ in range(2)]
        rT = [ar.bf(512) for _ in range(2)]
        w_up3 = w_up.rearrange("(c p) n -> p c n", p=128)
        w_dn3 = w_down.rearrange("(c p) n -> p c n", p=128)

        def load_group(g):
            s_ = g % 2
            dma_pool(wu[s_], w_up3[:, :, g * 512:(g + 1) * 512], "wu%d" % s_)
            dma_pool(wd[s_], w_dn3[:, 4 * g:4 * g + 4, :], "wd%d" % s_)
            tt("dve", wd[s_], wd[s_], g2b[:].unsqueeze(1).broadcast_to([128, 4, 1024]), ALU.mult, ["wd%d" % s_, "g2b"], ["wd%d" % s_])

        load_group(0)
        load_group(1)
        cntb = [0]

        def up(u):
            g, blk = u // 4, u % 4
            s_ = g % 2
            us = u % 2
            tok = slice(blk * 512, (blk + 1) * 512)
            h2k = ["h2T_%d_%d" % (t_, c_) for t_ in range(4 * blk, 4 * blk + 4) for c_ in range(8)]
            for fcl in range(4):
                pu = cntb[0] % 2
                cntb[0] += 1
                for c in range(8):
                    mm(PS(pu), wu[s_][:, c, fcl * 128:(fcl + 1) * 128], h2T[:, c, tok], c == 0, c == 7, ["wu%d" % s_] + h2k, [pk(pu)])
                act(rT[pu], PS(pu), AF.Relu, [pk(pu)], ["rT%d" % pu])
                tt("pool", uT[us][:, fcl, :], rT[pu], rT[pu], ALU.mult, ["rT%d" % pu], ["uT%d_%d" % (us, fcl)])

        def down(u):
            g, blk = u // 4, u % 4
            s_ = g % 2
            us = u % 2
            uk = ["uT%d_%d" % (us, f_) for f_ in range(4)]
            for i in range(4):
                ti_ = 4 * blk + i
                for half in range(2):
                    pd = 2 + (2 * i + half) % 4
                    for fcl in range(4):
                        mm(PS(pd), uT[us][:, fcl, i * 128:(i + 1) * 128], wd[s_][:, fcl, half * 512:(half + 1) * 512], fcl == 0, fcl == 3,
                           uk + ["wd%d" % s_], [pk(pd)])
                    hs = slice(half * 512, (half + 1) * 512)
                    tt("dve", x1[:, ti_, hs], PS(pd), x1[:, ti_, hs], ALU.add, [pk(pd), "x1_%d_%d" % (ti_, half)], ["x1_%d_%d" % (ti_, half)])
            if blk == 3 and g + 2 < 8:
                load_group(g + 2)

        for u in range(33):
            if u < 32:
                up(u)
            if u >= 1:
                down(u - 1)
        for i in range(16):
            dma_sp(out_d[i * 128:(i + 1) * 128, :], x1[:, i, :], "out%d" % i, reads=["x1_%d_0" % i, "x1_%d_1" % i])
            fin.append("out%d" % i)
        finish()
    return nc


def _host_tables(h):
    bf = ml_dtypes.bfloat16
    a = np.arange(64)
    k1 = 32 * h + np.arange(32)
    ph = 2 * np.pi * np.outer(a, k1) / 64.0
    W1h = np.concatenate([np.cos(ph), -np.sin(ph)], axis=1)
    W1 = np.zeros((128, 128), np.float64)
    W1[0:64, 0:64] = W1h
    W1[64:128, 64:128] = W1h
    W1 = W1.astype(np.float32).astype(bf)
    c = np.arange(128)
    pc = 2 * np.pi * np.outer(c, c) / 128.0
    C, Sn = np.cos(pc), np.sin(pc)
    CS = np.concatenate([C, -Sn, Sn, C], axis=1).astype(np.float32).astype(bf)
    nrm = 1.0 / np.sqrt(4096.0 * 128.0)
    T2 = np.zeros((128, 16, 2, 2, 64), np.float64)
    k2 = np.arange(64)
    for kk in range(2):
        for n2p in range(64):
            s_ = n2p // 32
            bl = n2p % 32
            n2 = 32 * (h ^ s_) + bl
            for pair in range(16):
                k1_ = 32 * h + 2 * pair + kk
                th = 2 * np.pi * ((n2 * (k1_ + 64 * k2)) % 4096) / 4096.0
                T2[64 * kk + n2p, pair, kk, 0, :] = np.cos(th) * nrm
                T2[64 * kk + n2p, pair, kk, 1, :] = np.sin(th) * nrm
    T2 = T2.reshape(128, 4096).astype(np.float32).astype(bf)
    freqs = (np.float32(10000.0) ** (-np.arange(8, dtype=np.float32) / np.float32(8))).astype(np.float32)
    cosK = np.ones((128, 34, 32), np.float32)
    sinK = np.zeros((128, 34, 32), np.float32)
    p = np.arange(128)
    q = p // 64
    aa = p % 64
    sign = np.concatenate([-np.ones(8), np.ones(8), -np.ones(8), np.ones(8)]).astype(np.float32)
    for kt in range(32):
        s_, j = kt // 16, kt % 16
        bl = 2 * j + q
        n = 64 * aa + 32 * (h ^ s_) + bl
        row = (n // 64).astype(np.float32)
        col = (n % 64).astype(np.float32)
        ang_r = row[:, None] * freqs[None, :]
        ang_c = col[:, None] * freqs[None, :]
        ang = np.concatenate([ang_r, ang_r, ang_c, ang_c], axis=-1).astype(np.float32)
        cosK[:, kt, :] = np.cos(ang)
        sinK[:, kt, :] = np.sin(ang) * sign[None, :]
    return W1, CS, T2, cosK.reshape(128, -1), sinK.reshape(128, -1)


def _swap32(g):
    return np.concatenate([g[8:16], g[0:8], g[24:32], g[16:24]])


_NC_CACHE = {}


def kernel(x, c, ctx, c_ctx, ada_w, ada_b, norm1_g, norm2_g, w_in, b_gate, w_fourier, q_norm_g, w_qb,
           kv_norm_g, w_kvb, q_gain, k_gain, w_mla_o, w_out, w_up, w_down, _debug=(), _upto=None, _cores=None):
    f = lambda a: np.ascontiguousarray(np.asarray(a, dtype=np.float32))
    x, c, ctx, c_ctx = f(x), f(c), f(ctx), f(c_ctx)
    key = (tuple(_debug), _upto)
    if key not in _NC_CACHE:
        _NC_CACHE[key] = build(debug=_debug, upto=_upto)
    nc = _NC_CACHE[key]

    def cols(v, n):
        return np.ascontiguousarray(f(v).reshape(n, 128).T)

    kg, qg = f(k_gain)[0], f(q_gain)[0]
    gk96 = np.zeros((128, 1), np.float32); gk96[:64, 0] = kg[:64]; gk96[64:96, 0] = 1.0
    gq96 = np.zeros((128, 1), np.float32); gq96[:64, 0] = qg[:64]; gq96[64:96, 0] = 1.0
    gkr = np.ascontiguousarray(np.broadcast_to(np.concatenate([kg[64:96], _swap32(kg[64:96])])[None, :], (128, 64)))
    gqr = np.ascontiguousarray(np.broadcast_to(np.concatenate([qg[64:96], _swap32(qg[64:96])])[None, :], (128, 64)))
    sel = np.zeros((2, 130), np.float32); sel[0, 0] = 1; sel[1, 1] = 1; sel[0, 2:] = 1
    common = {
        "ada_w": f(ada_w)[0], "ada_b2": np.ascontiguousarray(np.stack([f(ada_b)[0], f(ada_b)[0]])),
        "n1g": cols(norm1_g[0], 8), "n2g": cols(norm2_g[0], 8), "bgate": cols(b_gate[0], 16),
        "qng": cols(q_norm_g[0], 2), "kvng": cols(kv_norm_g[0], 1), "gk96": gk96, "gq96": gq96, "gkr": gkr, "gqr": gqr,
        "w_in": f(w_in)[0], "w_fourier": f(w_fourier)[0], "w_qb": f(w_qb)[0], "w_kvb": f(w_kvb)[0], "w_mla_o": f(w_mla_o)[0],
        "w_out": f(w_out)[0], "w_up": f(w_up)[0], "w_down": f(w_down)[0],
        "ident": np.eye(128, dtype=np.float32).astype(ml_dtypes.bfloat16), "sel": sel,
    }
    tabs = [_host_tables(h) for h in range(2)]
    in_maps = []
    for core in range(8):
        b, h = core // 2, core % 2
        xb = x[b].reshape(64, 64, D)
        own = xb[:, 32 * h:32 * h + 32, :].transpose(1, 0, 2).reshape(NOWN, D)
        oth = xb[:, 32 * (1 - h):32 * (1 - h) + 32, :].transpose(1, 0, 2).reshape(NOWN, D)
        cv = np.stack([c[b], c_ctx], -1).reshape(8, 128, 2).transpose(1, 0, 2).reshape(128, 16)
        W1, CS, T2, cosK, sinK = tabs[h]
        m = dict(common)
        m.update({"xp": np.ascontiguousarray(np.concatenate([own, oth], 0)), "ctx": ctx[b], "cvec": np.ascontiguousarray(cv),
                  "W1": W1, "CS": CS, "T2": T2, "cosK": cosK, "sinK": sinK})
        in_maps.append(m)
    if _cores is not None:
        res = run_bass_kernel_spmd(nc, [in_maps[c_] for c_ in _cores], core_ids=list(range(len(_cores))))
        return None, res
    res = run_bass_kernel_spmd(nc, in_maps, core_ids=list(range(8)))
    out = np.empty((4, S, D), np.float32)
    for core in range(8):
        b, h = core // 2, core % 2
        oc = np.asarray(res.results[core]["out"]).reshape(32, 64, D).transpose(1, 0, 2)
        out[b].reshape(64, 64, D)[:, 32 * h:32 * h + 32, :] = oc
    if _debug or _upto:
        return out, res
    return out
```
